# Optimizing a Trainium2 kernel written in Bass

```python
import jax, jax.numpy as jnp
from jax import lax
import numpy as np

D_MODEL = 1024
BATCH = 16
SEQ = 256
DEPTH = 4
DEC_BATCH = 2
DEC_SEQ = 4096
PAST_LEN = 256

GRID_W = 64
N_MIXERS = 2
N_FOURIER_LAYERS = (DEPTH + 1) // 2
N_NA_LAYERS = DEPTH // 2
N_HEADS = 16
HEAD_DIM = D_MODEL // N_HEADS
N_FOURIER_GROUPS = 8
FOURIER_GROUP_DIM = D_MODEL // N_FOURIER_GROUPS
WIN_ROWS_MAX = 8
WIN_COLS = 16
D_FF = 2816
N_MOD = 9
RMS_EPS = 1e-6

kernel_name = "hybrid_fnet_natten_macaron_step"


def _rmsnorm(x, g):
    x32 = x.astype(jnp.float32)
    y = x32 * lax.rsqrt(jnp.mean(x32 * x32, axis=-1, keepdims=True) + RMS_EPS)
    return (y * g.astype(jnp.float32)).astype(x.dtype)


def _modulation(cond, w_mod, b_mod):
    m = jax.nn.silu(cond) @ w_mod + b_mod
    return m.reshape(cond.shape[0], N_MOD, D_MODEL)


def _modulated_prenorm(x, g, m, k):
    return _rmsnorm(x, g) * (1 + m[:, 3 * k + 1, None]) + m[:, 3 * k, None]


def _swiglu(h, w1, w2):
    gate, up = jnp.split(h @ w1, 2, axis=-1)
    return (jax.nn.silu(gate) * up) @ w2


def _half_ffn(x, m, k, g_pre, g_post, w1, w2):
    h = _modulated_prenorm(x, g_pre, m, k)
    return x + 0.5 * m[:, 3 * k + 2, None] * _rmsnorm(_swiglu(h, w1, w2), g_post)


def _fourier_mix(h, w_in, w_out):
    b, s, _ = h.shape
    u = (h @ w_in).reshape(b, s, N_FOURIER_GROUPS, FOURIER_GROUP_DIM)
    f = jnp.fft.fft2(u.astype(jnp.float32), axes=(1, 3), norm="ortho").real
    return f.astype(h.dtype).reshape(b, s, D_MODEL) @ w_out


def _split_heads(t):
    b, s, _ = t.shape
    return t.reshape(b, s, N_HEADS, HEAD_DIM).transpose(0, 2, 1, 3)


def _merge_heads(t):
    b, h, s, dh = t.shape
    return t.transpose(0, 2, 1, 3).reshape(b, s, h * dh)


def _qkv(h, w_qkv):
    q, k, v = jnp.split(h @ w_qkv, 3, axis=-1)
    return _split_heads(q) * (HEAD_DIM ** -0.5), _split_heads(k), _split_heads(v)


def _context_attention(q, k, v):
    s = jnp.einsum('bhqd,bhkd->bhqk', q, k).astype(jnp.float32)
    p = jax.nn.softmax(s, axis=-1).astype(v.dtype)
    return jnp.einsum('bhqk,bhkd->bhqd', p, v)


def _neighbourhood_attention(q, k, v, k_ctx, v_ctx, rpb, rows):
    b, h, s, dh = q.shape
    kh = min(WIN_ROWS_MAX, rows)
    qg = q.reshape(b, h, rows, GRID_W, dh)
    kg = k.reshape(b, h, rows, GRID_W, dh)
    vg = v.reshape(b, h, rows, GRID_W, dh)
    cols = jnp.arange(GRID_W)
    col_start = jnp.clip(cols - WIN_COLS // 2, 0, GRID_W - WIN_COLS)
    col_idx = col_start[:, None] + jnp.arange(WIN_COLS)[None, :]
    col_off = col_idx - cols[:, None] + (WIN_COLS - 1)
    n_loc = kh * WIN_COLS

    def row_block(r):
        rs = jnp.clip(r - kh // 2, 0, rows - kh)
        q_r = lax.dynamic_index_in_dim(qg, r, axis=2, keepdims=False)
        k_rows = lax.dynamic_slice_in_dim(kg, rs, kh, axis=2)
        v_rows = lax.dynamic_slice_in_dim(vg, rs, kh, axis=2)
        k_win = jnp.take(k_rows, col_idx, axis=3)
        v_win = jnp.take(v_rows, col_idx, axis=3)
        bias_rows = lax.dynamic_slice_in_dim(rpb, rs - r + (WIN_ROWS_MAX - 1), kh, axis=1)
        bias = jnp.take(bias_rows, col_off, axis=2).transpose(0, 2, 1, 3)
        s_loc = jnp.einsum('bhwd,bhiwjd->bhwij', q_r, k_win).astype(jnp.float32) + bias[None].astype(jnp.float32)
        s_ctx = jnp.einsum('bhwd,bhpd->bhwp', q_r, k_ctx).astype(jnp.float32)
        logits = jnp.concatenate([s_loc.reshape(b, h, GRID_W, n_loc), s_ctx], axis=-1)
        p = jax.nn.softmax(logits, axis=-1).astype(v.dtype)
        p_loc = p[..., :n_loc].reshape(b, h, GRID_W, kh, WIN_COLS)
        p_ctx = p[..., n_loc:]
        return (jnp.einsum('bhwij,bhiwjd->bhwd', p_loc, v_win)
                + jnp.einsum('bhwp,bhpd->bhwd', p_ctx, v_ctx))

    o = lax.map(row_block, jnp.arange(rows))
    return o.transpose(1, 0, 3, 2, 4).reshape(b, s, h * dh)


def setup_inputs(seed: int = 0) -> dict:
    key = jax.random.key(seed)
    ks = jax.random.split(key, 18)
    f32 = jnp.float32

    def nrm(k, shape, scale):
        return jax.random.normal(k, shape, f32) * scale

    return {
        "x_prompt": nrm(ks[0], (BATCH, SEQ, D_MODEL), 1.0),
        "x_sample": nrm(ks[1], (DEC_BATCH, DEC_SEQ, D_MODEL), 1.0),
        "cache_k": nrm(ks[2], (DEC_BATCH, N_NA_LAYERS, N_HEADS, PAST_LEN, HEAD_DIM), 1.0),
        "cache_v": nrm(ks[3], (DEC_BATCH, N_NA_LAYERS, N_HEADS, PAST_LEN, HEAD_DIM), 1.0),
        "c": nrm(ks[4], (DEC_BATCH, D_MODEL), 1.0),
        "c_ctx": nrm(ks[5], (D_MODEL,), 1.0),
        "w_mod": nrm(ks[6], (DEPTH, D_MODEL, N_MOD * D_MODEL), 0.5 * D_MODEL ** -0.5),
        "b_mod": nrm(ks[7], (DEPTH, N_MOD * D_MODEL), 0.01),
        "norm_pre": 1.0 + nrm(ks[8], (DEPTH, 3, D_MODEL), 0.02),
        "norm_post": 1.0 + nrm(ks[9], (DEPTH, 3, D_MODEL), 0.02),
        "ffn_w1": nrm(ks[10], (DEPTH, 2, D_MODEL, 2 * D_FF), D_MODEL ** -0.5),
        "ffn_w2": nrm(ks[11], (DEPTH, 2, D_FF, D_MODEL), D_FF ** -0.5),
        "four_w_in": nrm(ks[12], (N_FOURIER_LAYERS, D_MODEL, D_MODEL), D_MODEL ** -0.5),
        "four_w_out": nrm(ks[13], (N_FOURIER_LAYERS, D_MODEL, D_MODEL), D_MODEL ** -0.5),
        "na_w_qkv": nrm(ks[14], (N_NA_LAYERS, D_MODEL, 3 * D_MODEL), D_MODEL ** -0.5),
        "na_w_out": nrm(ks[15], (N_NA_LAYERS, D_MODEL, D_MODEL), D_MODEL ** -0.5),
        "na_rpb": nrm(ks[16], (N_NA_LAYERS, N_HEADS, 2 * WIN_ROWS_MAX - 1, 2 * WIN_COLS - 1), 0.02),
    }


def reference(x_prompt, x_sample, cache_k, cache_v, c, c_ctx, w_mod, b_mod, norm_pre, norm_post,
              ffn_w1, ffn_w2, four_w_in, four_w_out, na_w_qkv, na_w_out, na_rpb):
    rows = x_sample.shape[1] // GRID_W
    xp = x_prompt
    xs = x_sample
    new_k = []
    new_v = []
    for i in range(DEPTH):
        m_ctx = _modulation(c_ctx[None, :], w_mod[i], b_mod[i])
        m_lat = _modulation(c, w_mod[i], b_mod[i])
        xp = _half_ffn(xp, m_ctx, 0, norm_pre[i, 0], norm_post[i, 0], ffn_w1[i, 0], ffn_w2[i, 0])
        xs = _half_ffn(xs, m_lat, 0, norm_pre[i, 0], norm_post[i, 0], ffn_w1[i, 0], ffn_w2[i, 0])
        hp = _modulated_prenorm(xp, norm_pre[i, 1], m_ctx, 1)
        hs = _modulated_prenorm(xs, norm_pre[i, 1], m_lat, 1)
        j = i // N_MIXERS
        if i % N_MIXERS == 0:
            yp = _fourier_mix(hp, four_w_in[j], four_w_out[j])
            ys = _fourier_mix(hs, four_w_in[j], four_w_out[j])
        else:
            qp, kp, vp = _qkv(hp, na_w_qkv[j])
            new_k.append(kp)
            new_v.append(vp)
            yp = _merge_heads(_context_attention(qp, kp, vp)) @ na_w_out[j]
            qs, ks_, vs = _qkv(hs, na_w_qkv[j])
            ys = _neighbourhood_attention(qs, ks_, vs, cache_k[:, j], cache_v[:, j], na_rpb[j], rows) @ na_w_out[j]
        xp = xp + m_ctx[:, 5, None] * _rmsnorm(yp, norm_post[i, 1])
        xs = xs + m_lat[:, 5, None] * _rmsnorm(ys, norm_post[i, 1])
        xp = _half_ffn(xp, m_ctx, 2, norm_pre[i, 2], norm_post[i, 2], ffn_w1[i, 1], ffn_w2[i, 1])
        xs = _half_ffn(xs, m_lat, 2, norm_pre[i, 2], norm_post[i, 2], ffn_w1[i, 1], ffn_w2[i, 1])
    new_cache_k = jnp.stack(new_k, axis=1)
    new_cache_v = jnp.stack(new_v, axis=1)
    return (xp, xs, new_cache_k, new_cache_v)
```

```python
import numpy as np
import concourse.bass as bass
import concourse.mybir as mybir
from concourse.bass_utils import run_bass_kernel_spmd

F32 = mybir.dt.float32
BF16 = mybir.dt.bfloat16
I32 = mybir.dt.int32
ALU = mybir.AluOpType
AF = mybir.ActivationFunctionType

_DSZ = {F32: 4, BF16: 2, I32: 4}


def _dsz(dt):
    try:
        return _DSZ[dt]
    except Exception:
        return mybir.dt.size(dt)


class _Op:
    __slots__ = ("idx", "eng", "method", "kwargs", "deps", "is_dma", "done", "waits",
                 "signals", "kad", "inc", "custom")


class Prog:
    ENGS = ("sync", "scalar", "vector", "gpsimd", "tensor")

    def __init__(self, nc, n_dma_sems=20, same_engine_sync=True):
        self.nc = nc
        self.ops = []
        self.recs = {}
        self.n_dma_sems = n_dma_sems
        self.dma_rr = {"h": 0, "g": 0}
        self.dma_last = {"h": [None] * n_dma_sems, "g": [None] * n_dma_sems}
        self.dma_cnt = {"h": [0] * n_dma_sems, "g": [0] * n_dma_sems}
        self.same_engine_sync = same_engine_sync
        self.psum_names = set()
        self.untracked = set()

    def region(self, ap):
        name = ap.name
        dims = ap.ap
        off = int(ap.offset)
        sp = str(ap.space)
        if sp in ("SB", "PSUM") or "SB" in sp or "PSUM" in sp:
            sz = _dsz(ap.dtype)
            pstep = dims[0][0]
            if name in self.psum_names:
                return (name, 0, 128, 0, 1 << 30)
            tshape = list(ap.tensor.shape)
            tsz = _dsz(ap.tensor.dtype)
            pstride = 1
            for s in tshape[1:]:
                pstride *= int(s)
            pstride = pstride * tsz // sz
            plo = off // pstride
            flo = off % pstride
            if pstep == 0:
                phi = plo + 1
            else:
                phi = plo + (dims[0][1] - 1) * (pstep // pstride) + 1
            fhi = flo + 1
            for st, c in dims[1:]:
                if st > 0:
                    fhi += (c - 1) * st
                elif st < 0:
                    flo += (c - 1) * st
            return (name, plo, phi, flo * sz, fhi * sz)
        else:
            sz = _dsz(ap.dtype)
            lo = off
            hi = off + 1
            for st, c in dims:
                if st > 0:
                    hi += (c - 1) * st
                elif st < 0:
                    lo += (c - 1) * st
            return (name, 0, 1, lo * sz, hi * sz)

    def add(self, eng, method, reads=(), writes=(), is_dma=False, custom=None, **kwargs):
        op = _Op()
        op.idx = len(self.ops)
        op.eng = eng
        op.method = method
        op.kwargs = kwargs
        op.is_dma = is_dma
        op.custom = custom
        op.deps = set()
        op.signals = False
        op.done = None
        op.waits = {}
        op.kad = None
        op.inc = 0
        rr = [self.region(a) for a in reads if a is not None]
        ww = [self.region(a) for a in writes if a is not None]
        rr = [r for r in rr if r[0] not in self.untracked]
        ww = [r for r in ww if r[0] not in self.untracked]
        ww = ww + [r for r in rr if r[0] in self.psum_names]
        rr = [r for r in rr if r[0] not in self.psum_names]
        for (name, plo, phi, flo, fhi) in rr:
            for rec in self.recs.get(name, ()):
                if rec[5] and rec[0] < phi and plo < rec[1] and rec[2] < fhi and flo < rec[3]:
                    op.deps.add(rec[4])
        for (name, plo, phi, flo, fhi) in ww:
            lst = self.recs.setdefault(name, [])
            keep = []
            for rec in lst:
                if rec[0] < phi and plo < rec[1] and rec[2] < fhi and flo < rec[3]:
                    op.deps.add(rec[4])
                    if plo <= rec[0] and rec[1] <= phi and flo <= rec[2] and rec[3] <= fhi:
                        continue
                keep.append(rec)
            keep.append([plo, phi, flo, fhi, op, True])
            self.recs[name] = keep
        for (name, plo, phi, flo, fhi) in rr:
            lst = self.recs.setdefault(name, [])
            rep = False
            if not is_dma:
                for rec in lst:
                    if (not rec[5]) and rec[0] == plo and rec[1] == phi and rec[2] == flo \
                            and rec[3] == fhi and rec[4].eng == eng and not rec[4].is_dma:
                        rec[4] = op
                        rep = True
                        break
            if not rep:
                lst.append([plo, phi, flo, fhi, op, False])
        op.deps.discard(op)
        if eng == "tensor":
            op.deps = {d for d in op.deps if d.eng != "tensor"}
        elif not self.same_engine_sync and not is_dma:
            op.deps = {d for d in op.deps if d.eng != eng or d.is_dma}
        if is_dma:
            pool = "g" if eng == "gpsimd" else "h"
            s = self.dma_rr[pool]
            self.dma_rr[pool] = (s + 1) % self.n_dma_sems
            prev = self.dma_last[pool][s]
            if prev is not None:
                op.deps.add(prev)
            self.dma_last[pool][s] = op
            self.dma_cnt[pool][s] += 16
            op.done = ("%s%d" % (pool, s), self.dma_cnt[pool][s])
            op.signals = True
            op.inc = 16
        for d in op.deps:
            d.signals = True
        self.ops.append(op)
        return op

    def _auto(self, eng, method, kw, extra_reads=(), extra_writes=()):
        reads = list(extra_reads)
        writes = list(extra_writes)
        for k, v in kw.items():
            if hasattr(v, "ap") and hasattr(v, "tensor"):
                if k in ("out", "accum_out", "ap"):
                    writes.append(v)
                else:
                    reads.append(v)
        return self.add(eng, method, reads=reads, writes=writes, **kw)

    def V(self, method, **kw):
        return self._auto("vector", method, kw)

    def G(self, method, **kw):
        return self._auto("gpsimd", method, kw)

    def A(self, method, **kw):
        return self._auto("scalar", method, kw)

    def mm(self, out, lhsT, rhs, start=True, stop=True, **kw):
        rd = [lhsT, rhs]
        return self.add("tensor", "matmul", reads=rd, writes=[out], out=out, lhsT=lhsT, rhs=rhs,
                        start=start, stop=stop, **kw)

    def transpose(self, out, in_, identity):
        return self.add("tensor", "transpose", reads=[in_, identity], writes=[out], out=out, in_=in_,
                        identity=identity)

    def dma(self, eng, out, in_, **kw):
        return self.add(eng, "dma_start", reads=[in_], writes=[out], is_dma=True, out=out, in_=in_, **kw)

    def emit(self):
        nc = self.nc
        cnt = {e: 0 for e in self.ENGS}
        for op in self.ops:
            if op.is_dma:
                continue
            if op.signals:
                cnt[op.eng] += 1
                op.done = ("e_" + op.eng, cnt[op.eng])
                op.inc = 1
        knows = {e: {} for e in self.ENGS}
        nwaits = 0
        for op in self.ops:
            kn = knows[op.eng]
            need = {}
            for p in sorted(op.deps, key=lambda o: o.idx):
                s, v = p.done
                if kn.get(s, 0) >= v:
                    continue
                if need.get(s, 0) < v:
                    need[s] = v
                kn[s] = v
                if p.kad is not None:
                    for s2, v2 in p.kad.items():
                        if kn.get(s2, 0) < v2:
                            kn[s2] = v2
            op.waits = need
            nwaits += len(need)
            if op.signals:
                kad = dict(kn)
                kad[op.done[0]] = op.done[1]
                op.kad = kad
        sem_names = ["e_" + e for e in self.ENGS] + ["h%d" % i for i in range(self.n_dma_sems)] + ["g%d" % i for i in range(self.n_dma_sems)]
        self.stats = dict(n_ops=len(self.ops), n_waits=nwaits,
                          per_eng={e: sum(1 for o in self.ops if o.eng == e) for e in self.ENGS})
        from contextlib import ExitStack
        with ExitStack() as st:
            sems = {n: st.enter_context(nc.semaphore(n)) for n in sem_names}
            block = st.enter_context(nc.Block())
            by_eng = {e: [o for o in self.ops if o.eng == e] for e in self.ENGS}

            def run(eng_obj, ops):
                for op in ops:
                    for s, v in op.waits.items():
                        eng_obj.wait_ge(sems[s], v)
                    if op.custom is not None:
                        ins = op.custom(eng_obj)
                    else:
                        ins = getattr(eng_obj, op.method)(**op.kwargs)
                    if op.signals and ins is not None:
                        ins.then_inc(sems[op.done[0]], op.inc)

            @block.sync
            def _(e):
                run(e, by_eng["sync"])

            @block.scalar
            def _(e):
                run(e, by_eng["scalar"])

            @block.vector
            def _(e):
                run(e, by_eng["vector"])

            @block.gpsimd
            def _(e):
                run(e, by_eng["gpsimd"])

            @block.tensor
            def _(e):
                run(e, by_eng["tensor"])


import ml_dtypes
from contextlib import ExitStack

D = 1024
DFF = 2816
NT = 1536
DEPTH = 4
EPS = 1e-6
NEG = -30000.0

WB_SZ = 11264
NWB = 4
MOD_PER_SLAB = [2, 2, 2, 2, 2, 2, 2, 1, 1, 1, 1]
HB_OFF = WB_SZ * NWB
HB_SZ = 24576
BIG_OFF = HB_OFF + HB_SZ
BIG_SZ = 69632
ARENA = BIG_OFF + BIG_SZ


def _tcs(tc):
    return slice(512 * tc, 512 * tc + 512)


class StopBuild(Exception):
    pass


class Builder:
    def __init__(self, n_layers=DEPTH):
        self.n_layers = n_layers
        nc = bass.Bass("TRN2", target_bir_lowering=False)
        self.nc = nc
        self.P = Prog(nc, n_dma_sems=16)
        self.st = ExitStack()

    def decl(self):
        nc = self.nc
        di = lambda n, s, d=F32: nc.dram_tensor(n, s, d, kind="ExternalInput")
        self.xin = di("xin", [NT, D])
        self.condT = di("condT", [128, 8, 2])
        self.w_mod = di("w_mod", [DEPTH, D, 9 * D])
        self.bmT = di("bmT", [128, DEPTH, 72])
        self.gpre = di("gpre", [128, DEPTH, 3, 8])
        self.gpost = di("gpost", [128, DEPTH, 3, 8])
        self.w1 = di("ffn_w1", [DEPTH, 2, D, 2 * DFF])
        self.w2 = di("ffn_w2", [DEPTH, 2, DFF, D])
        self.fwin = di("four_w_in", [2, D, D])
        self.fwout = di("four_w_out", [2, D, D])
        self.wqkv = di("na_w_qkv", [2, D, 3 * D])
        self.wout = di("na_w_out", [2, D, D])
        self.ckT = di("ckT", [2, 128, 8, 256])
        self.cv = di("cv", [2, 128, 2, 1024])
        self.rpbH = di("rpbH", [2 * 16 * 2 * 22 * 128])
        self.ident_d = di("ident", [128, 128])
        self.ccsc_d = di("ccsc", [128, 256])
        self.d256_d = di("d256", [128, 2, 2, 256])
        self.dfts_d = di("dfts", [32, 128, 2, 1024], BF16)
        self.j2_d = di("j2", [64, 128])
        self.colmask_d = di("colmask", [128, 64])
        self.maskt_d = di("maskt", [128, 2, 8, 8])
        self.gidx_d = di("gidx", [128, 12], I32)
        do = lambda n, s: nc.dram_tensor(n, s, F32, kind="ExternalOutput")
        self.yout = do("yout", [NT, D])
        self.nk = do("nk", [2, 2, 16, 256, 64])
        self.nv = do("nv", [2, 2, 16, 256, 64])
        self.ag_in = [nc.dram_tensor("ag_in0", [1024, 512], BF16)] * 2
        self.ag_out = [nc.dram_tensor("ag_out0", [4096, 512], BF16)] * 2
        self.vall = nc.dram_tensor("vall", [4096, 2048], BF16)

    def alloc(self):
        nc, st = self.nc, self.st
        sb = lambda n, s, d: st.enter_context(nc.sbuf_tensor(n, s, d))
        self.x = sb("x", [128, 8, NT], F32)
        self.arena = sb("arena", [128, ARENA // 2], BF16)
        self.ident = sb("identsb", [128, 128], F32)
        self.identb = sb("identb", [128, 128], BF16)
        self.ones = sb("ones", [128, 128], BF16)
        self.epsc = sb("epsc", [128, 1], F32)
        self.condsb = sb("condsb", [128, 8, 2], F32)
        self.scT = sb("scT", [128, 8, 2], BF16)
        self.bm = sb("bm", [128, DEPTH, 72], F32)
        self.gpre_s = sb("gpre_s", [128, DEPTH, 3, 8], F32)
        self.gpost_s = sb("gpost_s", [128, DEPTH, 3, 8], F32)
        self.modraw2 = [sb("modraw%d" % t, [128, 72, 2], F32) for t in range(2)]
        self.mA2 = [sb("mA%d" % t, [128, 3, 8, 2], F32) for t in range(2)]
        self.mG2 = [sb("mG%d" % t, [128, 3, 8, 2], F32) for t in range(2)]
        self.ccsc = sb("ccsc_s", [128, 256], BF16)
        self.d256 = sb("d256_s", [128, 2, 2, 256], BF16)
        self.j2 = sb("j2_s", [64, 128], F32)
        self.j2b = sb("j2b", [64, 128], BF16)
        self.maskt = sb("maskt_s", [128, 2, 8, 8], BF16)
        self.gidx = sb("gidx_s", [128, 12], I32)
        self.colmask_s = sb("colmask_s", [128, 64], F32)
        self.sq = [sb("sq%d" % i, [128, 512], BF16) for i in range(2)]
        self.rst = [sb("rst%d" % i, [128, 512], F32) for i in range(2)]
        self.tmp = [sb("tmp%d" % i, [128, 512], F32) for i in range(2)]
        self.ps = [st.enter_context(nc.psum_tensor("ps%d" % i, [128, 512], F32)) for i in range(8)]
        for p in self.ps:
            self.P.psum_names.add(p.name)
        self.cnt = {}

    def rot(self, key, n):
        v = self.cnt.get(key, 0)
        self.cnt[key] = v + 1
        return v % n

    def view(self, off, dt, *shape):
        n = 1
        for s in shape:
            n *= s
        assert off % 4 == 0
        if dt == BF16:
            ap = self.arena[:, off // 2: off // 2 + n]
        else:
            ap = self.arena[:, off // 2: off // 2 + 2 * n].bitcast(dt)
        if len(shape) == 2:
            ap = ap.rearrange("p (a b) -> p a b", a=shape[0])
        elif len(shape) == 3:
            ap = ap.rearrange("p (a b c) -> p a b c", a=shape[0], b=shape[1])
        return ap

    def wslot(self):
        i = self.rot("wb", NWB)
        return i * WB_SZ

    def ws_init(self, seq):
        self.ws_seq = seq
        self.ws_issued = 0
        self.ws_next = 0
        self.ws_slots = []
        self.gates = set()
        self.n_boot = 6

    def ws_issue_upto(self, n):
        while self.ws_issued < min(n, len(self.ws_seq)):
            kind, loads, gate = self.ws_seq[self.ws_issued]
            if gate is not None and gate not in self.gates:
                break
            if self.ws_issued < self.n_boot:
                off = (self.ws_issued % 14) * 8192
            else:
                off = (self.ws_issued % NWB) * WB_SZ
            self.ws_slots.append(off)
            for (eng, dstf, src) in loads:
                self.P.dma(eng, dstf(off), src)
            self.ws_issued += 1

    def ws_get(self, kind):
        i = self.ws_next
        assert self.ws_seq[i][0] == kind, (self.ws_seq[i][0], kind, i)
        if i < self.n_boot:
            self.ws_issue_upto(min(i + 14, self.n_boot))
        else:
            self.ws_issue_upto(i + NWB)
        assert self.ws_issued > i, (kind, i)
        self.ws_next += 1
        return self.ws_slots[i]

    def slab_views(self, off, nk, ncols):
        return self.view(off, BF16, nk, ncols)

    def build_ws_seq(self):
        seq = []
        V = self.view

        def wslab(kind, src_list, nk, ncols):
            loads = []
            for (src, c0) in src_list:
                c = src.shape[1]
                srcv = src.rearrange("(k p) n -> p k n", p=128)
                loads.append(("gpsimd",
                              (lambda off, c0=c0, c=c, nk=nk, ncols=ncols: V(off, BF16, nk, ncols)[:, :, c0:c0 + c]),
                              srcv))
            seq.append((kind, loads, None))

        def mod_slabs(i):
            wm = self.w_mod.ap()[i]
            for s in range(6):
                wslab("mod", [(wm[:, 512 * s:512 * s + 512], 0)], 8, 512)

        def ffn_slabs(i, a):
            w1 = self.w1.ap()[i, a]
            nxt = (i + 1 < self.n_layers) and not (a == 1 and i % 2 == 0)
            ms = 9 * a
            for sl in range(11):
                wslab("w1", [(w1[:, 256 * sl:256 * sl + 256], 0),
                             (w1[:, DFF + 256 * sl:DFF + 256 * sl + 256], 256)], 8, 512)
                if nxt:
                    for _ in range(1 if sl < 9 else 0):
                        wm = self.w_mod.ap()[i + 1]
                        wslab("mod", [(wm[:, 512 * ms:512 * ms + 512], 0)], 8, 512)
                        ms += 1
                if i == 0 and a == 0:
                    wm0 = self.w_mod.ap()[0]
                    for s0 in ([6 + sl] + ([17] if sl == 10 else [])):
                        wslab("mod", [(wm0[:, 512 * s0:512 * s0 + 512], 0)], 8, 512)
            w2 = self.w2.ap()[i, a]
            for s in range(4):
                wslab("w2", [(w2[:, 256 * s:256 * s + 256], 0)], 22, 256)

        def sq_slabs(kind, w, n, order=None):
            for s in (order if order is not None else range(n)):
                wslab(kind, [(w[:, 512 * s:512 * s + 512], 0)], 8, 512)

        for i in range(self.n_layers):
            if i == 0:
                mod_slabs(i)
            ffn_slabs(i, 0)
            j = i // 2
            if i % 2 == 0:
                sq_slabs("win", self.fwin.ap()[j], 2)
                if i + 1 < self.n_layers:
                    wm = self.w_mod.ap()[i + 1]
                    for ms in range(9, 18):
                        wslab("mod", [(wm[:, 512 * ms:512 * ms + 512], 0)], 8, 512)
                for ph in range(2):
                    for tt in range(32):
                        loads = [
                            ("scalar", (lambda off: V(off, BF16, 1024)),
                             self.vall.ap()[128 * tt:128 * tt + 128, 1024 * ph:1024 * ph + 1024]),
                            ("scalar", (lambda off: V(off + 2048, BF16, 2, 1024)),
                             self.dfts_d.ap()[tt]),
                        ]
                        seq.append(("dft", loads, ("ag", i, ph)))
                sq_slabs("wout", self.fwout.ap()[j], 2)
            else:
                sq_slabs("wqkv", self.wqkv.ap()[j], 6, order=(2, 3, 4, 5, 0, 1))
                sq_slabs("wout", self.wout.ap()[j], 2)
            ffn_slabs(i, 1)
        return seq

    def load_consts(self):
        P = self.P
        P.dma("sync", self.ident[:], self.ident_d.ap())
        P.V("tensor_copy", out=self.identb[:], in_=self.ident[:])
        P.V("memset", ap=self.ones[:], constant=1.0)
        P.V("memset", ap=self.epsc[:], constant=EPS)
        P.dma("sync", self.condsb[:], self.condT.ap())
        P.A("activation", out=self.scT[:], in_=self.condsb[:], func=AF.Silu)
        P.dma("sync", self.bm[:], self.bmT.ap())
        P.dma("sync", self.gpre_s[:], self.gpre.ap())
        P.dma("sync", self.gpost_s[:], self.gpost.ap())
        P.dma("gpsimd", self.ccsc[:], self.ccsc_d.ap())
        P.dma("gpsimd", self.d256[:], self.d256_d.ap())
        P.dma("sync", self.j2[:], self.j2_d.ap())
        P.dma("gpsimd", self.maskt[:], self.maskt_d.ap())
        P.dma("gpsimd", self.j2b[:], self.j2_d.ap())
        P.dma("sync", self.gidx[:], self.gidx_d.ap())
        P.dma("sync", self.colmask_s[:], self.colmask_d.ap())

    def load_x(self):
        P = self.P
        xin = self.xin.ap()
        for tt in range(12):
            so = ARENA - 8192 + (tt % 2) * 4096
            stg = self.view(so, F32, 1024)
            P.dma("sync", stg, xin[128 * tt:128 * tt + 128, :])
            for half in range(2):
                ps = self.ps[self.rot("ldps", 4)]
                for kk in range(4):
                    k = 4 * half + kk
                    P.transpose(ps[:, 128 * kk:128 * kk + 128], stg[:, 128 * k:128 * k + 128], self.ident[:])
                P.V("tensor_copy", out=self.x[:, 4 * half:4 * half + 4, 128 * tt:128 * tt + 128],
                    in_=ps[:].rearrange("p (a b) -> p a b", a=4))

    def store_x(self):
        P = self.P
        yo = self.yout.ap()
        outs = []
        for tt in range(12):
            so = BIG_OFF + (tt % 2) * 4096
            stg = self.view(so, F32, 1024)
            for half in range(2):
                ps = self.ps[self.rot("ldps", 4)]
                for kk in range(4):
                    k = 4 * half + kk
                    P.transpose(ps[:, 128 * kk:128 * kk + 128], self.x[:, k, 128 * tt:128 * tt + 128], self.ident[:])
                if half == 0:
                    P.V("tensor_copy", out=stg[:, 0:512], in_=ps[:])
                else:
                    P.A("activation", out=stg[:, 512:1024], in_=ps[:], func=AF.Copy)
            P.dma("sync", yo[128 * tt:128 * tt + 128, :], stg)
        self.out_aps.append(yo)

    def mod_step(self, i, s):
        P = self.P
        ps = self.ps[7]
        off = self.ws_get("mod")
        slab = self.view(off, BF16, 8, 512)
        for jj in range(4):
            for k in range(8):
                P.mm(ps[:, 2 * jj:2 * jj + 2], slab[:, k, 128 * jj:128 * jj + 128], self.scT[:, k, :],
                     start=(k == 0), stop=(k == 7), skip_group_check=True)
        modraw = self.modraw2[i % 2]
        P.V("tensor_tensor", out=modraw[:, 4 * s:4 * s + 4, :],
            in0=ps[:, 0:8].rearrange("p (j c) -> p j c", c=2),
            in1=self.bm[:, i, 4 * s:4 * s + 4].unsqueeze(2).broadcast_to([128, 4, 2]), op=ALU.add)

    def mod_finish(self, i, ss=(0, 1, 2)):
        P = self.P
        modraw, mA, mG = self.modraw2[i % 2], self.mA2[i % 2], self.mG2[i % 2]
        for s in ss:
            for c in range(2):
                P.V("scalar_tensor_tensor", out=mA[:, s, :, c],
                    in0=modraw[:, (3 * s + 1) * 8:(3 * s + 1) * 8 + 8, c], scalar=1.0,
                    in1=self.gpre_s[:, i, s, :], op0=ALU.add, op1=ALU.mult)
                P.V("scalar_tensor_tensor", out=mG[:, s, :, c],
                    in0=modraw[:, (3 * s + 2) * 8:(3 * s + 2) * 8 + 8, c],
                    scalar=(1.0 if s == 1 else 0.5),
                    in1=self.gpost_s[:, i, s, :], op0=ALU.mult, op1=ALU.mult)

    def set_layer(self, i):
        self.modraw, self.mA, self.mG = self.modraw2[i % 2], self.mA2[i % 2], self.mG2[i % 2]

    def modulation(self, i):
        for s in range(6):
            self.mod_step(i, s)
        self.mod_finish(i, ss=(0,))

    def mB(self, s, k, c):
        return self.modraw[:, (3 * s) * 8 + k, c:c + 1]

    def rstd_from(self, ps, r):
        P = self.P
        P.A("activation", out=r[:], in_=ps[:], func=AF.Ln, bias=self.epsc[:, 0:1], scale=1.0 / D)
        P.A("activation", out=r[:], in_=r[:], func=AF.Exp, scale=-0.5)

    def post_tc(self, pend, tc):
        P = self.P
        ybuf, ssb, s, mG = pend
        c = 0 if tc == 0 else 1
        r = self.rst[self.rot("rst", 2)]
        self.rstd_from(ssb[tc], r)
        for k in range(8):
            t = self.tmp[self.rot("tmp", 2)]
            P.V("scalar_tensor_tensor", out=t[:], in0=ybuf[:, k, _tcs(tc)], scalar=mG[:, s, k, c:c + 1],
                in1=r[:], op0=ALU.mult, op1=ALU.mult)
            P.V("tensor_tensor", out=self.x[:, k, _tcs(tc)], in0=self.x[:, k, _tcs(tc)], in1=t[:], op=ALU.add)

    def flush_post(self):
        pend = getattr(self, "pending_post", None)
        self.pending_post = None
        if pend is not None:
            for tc in range(3):
                self.post_tc(pend, tc)

    def prenorm(self, s):
        P = self.P
        h = self.view(HB_OFF, BF16, 8, NT)
        pend = getattr(self, "pending_post", None)
        self.pending_post = None
        for tc in range(3):
            if pend is not None:
                self.post_tc(pend, tc)
            c = 0 if tc == 0 else 1
            ps = self.ps[7]
            for k in range(8):
                sq = self.sq[self.rot("sq", 2)]
                if k % 2 == 0:
                    P.A("activation", out=sq[:], in_=self.x[:, k, _tcs(tc)], func=AF.Square)
                else:
                    P.V("tensor_tensor", out=sq[:], in0=self.x[:, k, _tcs(tc)], in1=self.x[:, k, _tcs(tc)], op=ALU.mult)
                P.mm(ps[:], self.ones[:], sq[:], start=(k == 0), stop=(k == 7))
            r = self.rst[self.rot("rst", 2)]
            self.rstd_from(ps, r)
            for k in range(8):
                t = self.tmp[self.rot("tmp", 2)]
                P.V("scalar_tensor_tensor", out=t[:], in0=self.x[:, k, _tcs(tc)], scalar=self.mA[:, s, k, c:c + 1],
                    in1=r[:], op0=ALU.mult, op1=ALU.mult)
                P.A("activation", out=h[:, k, _tcs(tc)], in_=t[:], func=AF.Identity, bias=self.mB(s, k, c), scale=1.0)
        return h

    def proj_fm(self, kind, nslab, ncol_chunks, nk, rhs_fn, ybuf, s):
        P = self.P
        ssb = [self.ps[4], self.ps[5], self.ps[6]]
        pend_ss = None
        for sl in range(nslab):
            off = self.ws_get(kind)
            slab = self.view(off, BF16, nk, 128 * ncol_chunks)
            for tc in range(3):
                for dd in range(ncol_chunks):
                    d = ncol_chunks * sl + dd
                    ps = self.ps[self.rot("py", 4)]
                    for k in range(nk):
                        P.mm(ps[:], slab[:, k, 128 * dd:128 * dd + 128], rhs_fn(k, tc), start=(k == 0), stop=(k == nk - 1))
                    if pend_ss is not None:
                        P.mm(*pend_ss[0], **pend_ss[1])
                    P.V("tensor_copy", out=ybuf[:, d, _tcs(tc)], in_=ps[:])
                    sq = self.sq[self.rot("sq", 2)]
                    P.A("activation", out=sq[:], in_=ps[:], func=AF.Square)
                    pend_ss = ((ssb[tc][:], self.ones[:], sq[:]), dict(start=(d == 0), stop=(d == 7), skip_group_check=True))
                if sl == nslab - 1:
                    P.mm(*pend_ss[0], **pend_ss[1])
                    pend_ss = None
                    self.post_tc((ybuf, ssb, s, self.mG), tc)
        self.pending_post = None

    def ffn(self, i, a, s):
        import os
        P = self.P
        h = self.prenorm(s)
        if os.environ.get("KSUB", "") == "pre":
            self.dbg_dump(h.rearrange("p a b -> p (a b)"), 8 * NT, BF16)
            raise StopBuild()
        gT = self.view(BIG_OFF, BF16, 22, NT)
        for sl in range(11):
            off = self.ws_get("w1")
            slab = self.view(off, BF16, 8, 512)
            for tc in range(3):
                for pp in range(2):
                    f = 2 * sl + pp
                    r = self.rot("pg", 3)
                    pg, pu = self.ps[2 * r], self.ps[2 * r + 1]
                    for k in range(8):
                        P.mm(pg[:], slab[:, k, 128 * pp:128 * pp + 128], h[:, k, _tcs(tc)], start=(k == 0), stop=(k == 7))
                    for k in range(8):
                        P.mm(pu[:], slab[:, k, 256 + 128 * pp:256 + 128 * pp + 128], h[:, k, _tcs(tc)],
                             start=(k == 0), stop=(k == 7))
                    t = self.tmp[self.rot("tmp", 2)]
                    P.A("activation", out=t[:], in_=pg[:], func=AF.Silu)
                    P.V("tensor_tensor", out=gT[:, f, _tcs(tc)], in0=t[:], in1=pu[:], op=ALU.mult)
            if i + 1 < self.n_layers and sl < 9 and not (a == 1 and i % 2 == 0):
                self.mod_step(i + 1, 9 * a + sl)
            if i == 0 and a == 0:
                self.mod_step(0, 6 + sl)
                if sl == 10:
                    self.mod_step(0, 17)
                    self.mod_finish(0, ss=(1, 2))
        if a == 1 and i + 1 < self.n_layers:
            self.mod_finish(i + 1)
        if os.environ.get("KSUB", "") == "w1":
            self.dbg_dump(gT.rearrange("p a b -> p (a b)"), 22 * NT, BF16)
            raise StopBuild()
        ybuf = h
        self.proj_fm("w2", 4, 2, 22, lambda k, tc: gT[:, k, _tcs(tc)], ybuf, s)

    def fourier(self, i):
        P = self.P
        j = i // 2
        h = self.prenorm(1)
        Vb = self.view(BIG_OFF, BF16, 12, 2048)
        UO = BIG_OFF + 49152
        pend_u = []

        def chan_dft(u, tc, g):
            for hh in range(2):
                pv = self.ps[4 + self.rot("pv", 2)]
                for tl in range(2):
                    t4 = 2 * hh + tl
                    P.mm(pv[:, 256 * tl:256 * tl + 256], u[:, 128 * t4:128 * t4 + 128], self.ccsc[:],
                         start=True, stop=True, skip_group_check=True)
                tt0 = 4 * tc + 2 * hh
                P.V("tensor_copy", out=Vb[:, tt0:tt0 + 2, 256 * g:256 * g + 256],
                    in_=pv[:].rearrange("p (a b) -> p a b", a=2))

        for sl in range(2):
            off = self.ws_get("win")
            slab = self.view(off, BF16, 8, 512)
            for gg in range(4):
                g = 4 * sl + gg
                for tc in range(3):
                    ps = self.ps[self.rot("py", 4)]
                    for k in range(8):
                        P.mm(ps[:], slab[:, k, 128 * gg:128 * gg + 128], h[:, k, _tcs(tc)], start=(k == 0), stop=(k == 7))
                    if pend_u:
                        chan_dft(*pend_u.pop(0))
                    u = self.view(UO + 1024 * self.rot("u", 4), BF16, 512)
                    P.A("activation", out=u, in_=ps[:], func=AF.Copy)
                    pend_u.append((u, tc, g))
                if g % 2 == 1:
                    while pend_u:
                        chan_dft(*pend_u.pop(0))
                    r = g // 2
                    P.dma("sync", self.ag_in[0].ap().rearrange("(a p) n -> p a n", p=128),
                          Vb[:, 4:12, 512 * r:512 * r + 512])
                    self.allgather(r, q="sync")
                    if r % 2 == 1:
                        self.gates.add(("ag", i, r // 2))
        fT = h
        if i + 1 < self.n_layers:
            for ms in range(9, 18):
                self.mod_step(i + 1, ms)
        for sq_ in range(2):
            for g in range(8):
                ps = self.ps[self.rot("py", 4)]
                n = 0
                for a in range(2):
                    for cs in range(2):
                        P.mm(ps[:, 0:256], Vb[:, 2 * sq_ + a, 256 * g + 128 * cs:256 * g + 128 * cs + 128],
                             self.d256[:, a, cs, :], start=(n == 0), stop=(n == 3))
                        n += 1
                P.V("tensor_copy", out=fT[:, g, 256 * sq_:256 * sq_ + 256], in_=ps[:, 0:256])
        for ph in range(2):
            for tt in range(32):
                off = self.ws_get("dft")
                vt = self.view(off, BF16, 1024)
                cst = self.view(off + 2048, BF16, 2, 1024)
                for gg in range(4):
                    for tq in range(2):
                        ps = self.ps[2 * gg + tq]
                        P.mm(ps[:], vt[:, 256 * gg:256 * gg + 128], cst[:, 0, 512 * tq:512 * tq + 512],
                             start=(tt == 0), stop=False)
                        P.mm(ps[:], vt[:, 256 * gg + 128:256 * gg + 256], cst[:, 1, 512 * tq:512 * tq + 512],
                             start=False, stop=(tt == 31))
            for gg in range(4):
                for tq in range(2):
                    ps = self.ps[2 * gg + tq]
                    o = fT[:, 4 * ph + gg, 512 + 512 * tq:512 + 512 * tq + 512]
                    if tq == 0:
                        P.V("tensor_copy", out=o, in_=ps[:])
                    else:
                        P.A("activation", out=o, in_=ps[:], func=AF.Copy)
        ybuf = self.view(BIG_OFF, BF16, 8, NT)
        self.proj_fm("wout", 2, 4, 8, lambda k, tc: fT[:, k, _tcs(tc)], ybuf, 1)

    def allgather(self, r, q="sync"):
        P = self.P
        ai, ao = self.ag_in[r % 2].ap(), self.ag_out[r % 2].ap()

        def cc(e):
            return e.collective_compute("AllGather", ALU.bypass, replica_groups=[[0, 1, 2, 3], [4, 5, 6, 7]],
                                        ins=[ai], outs=[ao])
        P.add("gpsimd", "cc", reads=[ai], writes=[ao], custom=cc)
        P.dma(q, self.vall.ap()[:, 512 * r:512 * r + 512], ao)

    def attention(self, i):
        P = self.P
        j = i // 2
        h = self.prenorm(1)
        B = BIG_OFF
        qT = self.view(B, BF16, 8, NT)
        Vext = self.view(B + 24576, BF16, 12, 1024)
        kTp = self.view(B + 49152, BF16, 8, 512)
        Vp = self.view(B + 57344, BF16, 4, 1024)
        kTc = self.view(B + 49152, BF16, 8, 256)
        Vc = self.view(B + 53248, BF16, 2, 1024)
        kText = self.view(HB_OFF, BF16, 8, NT)
        colmask = self.colmask_s[:].unsqueeze(1).broadcast_to([128, 22, 64])
        Ebuf = [self.sq[0][:], self.sq[1][:]]
        lg = [self.tmp[0][:], self.tmp[1][:]]
        rec = self.rst[0]
        stag32 = [self.view(B + 24576 + 2048 * t, F32, 512) for t in range(2)]
        stagb = [self.view(B + 24576 + 4096 + 1024 * t, BF16, 512) for t in range(2)]

        nk, nv = self.nk.ap(), self.nv.ap()
        for sl in (2, 3, 4, 5, 0, 1):
            off = self.ws_get("wqkv")
            slab = self.view(off, BF16, 8, 512)
            which = sl // 2
            half = sl % 2
            if which == 0 or which == 1:
                for tc in range(3 if which == 0 else 1):
                    for dd in range(4):
                        hp = 4 * half + dd
                        ps = self.ps[self.rot("py", 4)]
                        for k in range(8):
                            P.mm(ps[:], slab[:, k, 128 * dd:128 * dd + 128], h[:, k, _tcs(tc)], start=(k == 0), stop=(k == 7))
                        if which == 0:
                            P.A("activation", out=qT[:, hp, _tcs(tc)], in_=ps[:], func=AF.Copy, scale=0.125)
                        else:
                            P.V("tensor_copy", out=kTp[:, hp, 0:512], in_=ps[:])
            if which >= 1:
                dst = nk if which == 1 else nv
                for tt in range(12):
                    ps = self.ps[self.rot("py", 4)]
                    for k in range(8):
                        P.mm(ps[:], h[:, k, 128 * tt:128 * tt + 128], slab[:, k, :], start=(k == 0), stop=(k == 7))
                    if tt < 4:
                        s32 = stag32[self.rot("s32", 2)]
                        P.V("tensor_copy", out=s32, in_=ps[:])
                        sq_, t0 = tt // 2, 128 * (tt % 2)
                        d_ap = dst[sq_, j, 8 * half:8 * half + 8, t0:t0 + 128, :].rearrange("h t d -> t h d")
                        P.dma("sync", d_ap, s32.rearrange("p (h d) -> p h d", h=8))
                        if which == 2:
                            P.A("activation", out=Vp[:, tt, 512 * half:512 * half + 512], in_=ps[:], func=AF.Copy)
                    else:
                        sb_ = stagb[self.rot("sbf", 2)]
                        P.A("activation", out=sb_, in_=ps[:], func=AF.Copy)
                        P.dma("sync", self.ag_in[0].ap()[128 * (tt - 4):128 * (tt - 4) + 128, :], sb_)
                self.allgather((which - 1) * 2 + half)
        self.out_aps.append(nk)
        self.out_aps.append(nv)

        def attn_tail(pO, pD, rows, n, qslice):
            rc = self.rst[self.rot("rst", 2)][rows, 0:n]
            P.A("activation", out=rc, in_=pD[rows, 0:n], func=AF.Ln)
            P.A("activation", out=rc, in_=rc, func=AF.Exp, scale=-1.0)
            P.V("tensor_tensor", out=qslice, in0=pO[rows, 0:n], in1=rc, op=ALU.mult)

        tb0 = self.tmp[0][:].bitcast(BF16)
        Ep = [self.sq[0][:], self.sq[1][:], tb0[:, 0:512], tb0[:, 512:1024]]
        units = [(sq_, hp, e) for sq_ in range(2) for hp in range(8) for e in range(2)]

        def p_sgroup(u):
            sq_, hp, e = u
            rows = slice(64 * e, 64 * e + 64)
            qs = qT[rows, hp, 256 * sq_:256 * sq_ + 256]
            pS = self.ps[self.rot("pS", 4)]
            for kc in range(2):
                P.mm(pS[:, 256 * kc:256 * kc + 256], kTp[rows, hp, 256 * sq_ + 128 * kc:256 * sq_ + 128 * kc + 128], qs,
                     start=True, stop=True, skip_group_check=True)
            return pS

        pend = [p_sgroup(units[0]), p_sgroup(units[1])]
        for ui, u in enumerate(units):
            sq_, hp, e = u
            rows = slice(64 * e, 64 * e + 64)
            qs = qT[rows, hp, 256 * sq_:256 * sq_ + 256]
            if ui + 2 < len(units):
                pend.append(p_sgroup(units[ui + 2]))
            pS = pend[ui]
            r = self.rot("po", 2)
            pO, pD = self.ps[4 + 2 * r], self.ps[5 + 2 * r]
            E = Ep[self.rot("Ep", 4)]
            P.A("activation", out=E, in_=pS[:], func=AF.Exp)
            for kc in range(2):
                P.mm(pO[:, 0:256], Vp[:, 2 * sq_ + kc, 128 * hp:128 * hp + 128], E[:, 256 * kc:256 * kc + 256],
                     start=(kc == 0), stop=(kc == 1))
                P.mm(pD[:, 0:256], self.ones[:], E[:, 256 * kc:256 * kc + 256], start=(kc == 0), stop=(kc == 1))
            attn_tail(pO, pD, rows, 256, qs)

        ao = self.vall.ap()
        kvg0 = self.view(B + 49152, BF16, 2048)
        kvg1 = self.view(B + 53248, BF16, 2048)
        kvgs = [kvg0, kvg1]
        for et in range(12):
            kv = kvgs[et % 2]

            def gather(e, kv=kv, et=et):
                return e.indirect_dma_start(out=kv, out_offset=None, in_=ao,
                                            in_offset=bass.IndirectOffsetOnAxis(ap=self.gidx[:, et:et + 1], axis=0))
            P.add("gpsimd", "gather", reads=[ao, self.gidx[:, et:et + 1]], writes=[kv], custom=gather, is_dma=True)
            for half in range(2):
                ps = self.ps[self.rot("py", 4)]
                psb = ps[:].bitcast(BF16)
                for kk in range(4):
                    hp = 4 * half + kk
                    P.transpose(psb[:, 128 * kk:128 * kk + 128], kv[:, 128 * hp:128 * hp + 128], self.identb[:])
                P.V("tensor_copy", out=kText[:, 4 * half:4 * half + 4, 128 * et:128 * et + 128],
                    in_=psb[:, 0:512].rearrange("p (a b) -> p a b", a=4))
            P.A("activation", out=Vext[:, et, :], in_=kv[:, 1024:2048], func=AF.Copy)
        P.dma("gpsimd", kTc, self.ckT.ap()[j])
        P.dma("gpsimd", Vc, self.cv.ap()[j])

        T2rb = [self.view(B + 57344, BF16, 22, 64), self.view(B + 57344 + 2816, BF16, 22, 64)]
        Hkb = self.view(B + 57344 + 5632, BF16, 2, 16, 64)
        t0b = self.tmp[0][:].bitcast(BF16)
        t1b = self.tmp[1][:].bitcast(BF16)
        Ebuf = [self.sq[0][:], self.sq[1][:], t0b[:, 0:512], t0b[:, 512:1024], t1b[:, 0:512], t1b[:, 512:1024]]
        NE = len(Ebuf)
        P.V("memset", ap=T2rb[0], constant=0.0)
        P.V("memset", ap=T2rb[1], constant=0.0)
        cm8 = self.colmask_s[:].unsqueeze(1).broadcast_to([128, 8, 64])

        def build_steps(hd, T2r):
            for e2 in range(2):
                base = ((j * 16 + hd) * 2 + e2) * 22 * 128 + 3 * 128
                hap = bass.AP(self.rpbH, base, [[1, 64], [128, 16], [1, 64]])
                P.dma("gpsimd", Hkb[0:64, e2], hap)
            steps = []
            for e2 in range(2):
                for half in range(2):
                    def step(e2=e2, half=half, T2r=T2r):
                        ps = self.ps[self.rot("pS", 4)]
                        r2 = slice(64 * e2, 64 * e2 + 64)
                        P.mm(ps[:], self.j2b[:], Hkb[0:64, e2, 8 * half:8 * half + 8, :], start=True, stop=True)
                        P.V("tensor_tensor", out=T2r[r2, 3 + 8 * half:11 + 8 * half, :],
                            in0=ps[r2, :].rearrange("p (a b) -> p a b", a=8), in1=cm8[r2], op=ALU.add)
                    steps.append(step)
            return steps

        for st_ in build_steps(0, T2rb[0]):
            st_()
        for hd in range(16):
            hp, e = hd // 2, hd % 2
            rows = slice(64 * e, 64 * e + 64)
            T2r = T2rb[hd % 2]
            steps = build_steps(hd + 1, T2rb[(hd + 1) % 2]) if hd + 1 < 16 else []
            for b in range(2):
                QB = slice(512 + 512 * b, 1024 + 512 * b)
                qs = qT[rows, hp, QB]
                r = self.rot("po", 2)
                pO, pD = self.ps[4 + 2 * r], self.ps[5 + 2 * r]

                def sgroup(c, b=b, qs=qs, rows=rows, hp=hp, T2r=T2r):
                    pS = self.ps[self.rot("pS", 4)]
                    if c < 8:
                        ks = slice(512 * b + 128 * c, 512 * b + 128 * c + 128)
                        P.mm(pS[:], kText[rows, hp, ks], qs, start=True, stop=False)
                        P.mm(pS[:], self.identb[:], T2r[:, 14 - 2 * c:22 - 2 * c, :], start=False, stop=True)
                    else:
                        cc_ = c - 8
                        P.mm(pS[:], kTc[rows, hp, 128 * cc_:128 * cc_ + 128], qs, start=True, stop=True)
                    return pS

                pend = [sgroup(0), sgroup(1), sgroup(2)]
                for c in range(10):
                    if c + 3 < 10:
                        pend.append(sgroup(c + 3))
                    pS = pend[c]
                    E = Ebuf[self.rot("E", NE)]
                    P.A("activation", out=E, in_=pS[:], func=AF.Exp)
                    if c < 8:
                        E3 = E.rearrange("p (a b) -> p a b", a=8)
                        P.V("tensor_tensor", out=E3, in0=E3,
                            in1=self.maskt[:, b, c, :].unsqueeze(2).broadcast_to([128, 8, 64]), op=ALU.mult)
                    vsrc = Vext[:, 4 * b + c, 128 * hp:128 * hp + 128] if c < 8 else Vc[:, c - 8, 128 * hp:128 * hp + 128]
                    P.mm(pO[:], vsrc, E, start=(c == 0), stop=(c == 9))
                    P.mm(pD[:], self.ones[:], E, start=(c == 0), stop=(c == 9))
                for _ in range(2):
                    if steps:
                        steps.pop(0)()
                attn_tail(pO, pD, rows, 512, qs)
            while steps:
                steps.pop(0)()
        ybuf = self.view(B + 24576, BF16, 8, NT)
        self.proj_fm("wout", 2, 4, 8, lambda k, tc: qT[:, k, _tcs(tc)], ybuf, 1)

    def dbg_dump(self, ap, n, dt):
        d = self.nc.dram_tensor("dbg", [128, n], dt, kind="ExternalOutput")
        self.P.dma("sync", d.ap(), ap)
        self.out_aps.append(d.ap())

    def build(self):
        self.decl()
        self.alloc()
        self.out_aps = []
        self.ws_init(self.build_ws_seq())
        self.load_consts()
        self.load_x()
        import os
        stage = int(os.environ.get("KSTAGE", "9"))
        try:
            self.layers(stage)
        except StopBuild:
            pass
        self.flush_post()
        self.store_x()
        self.P.add("sync", "fence", reads=self.out_aps, custom=lambda e: None)
        self.P.emit()
        self.st.close()
        return self.nc

    def layers(self, stage):
        import os
        for i in range(self.n_layers):
            if stage < 1:
                break
            if i == 0:
                self.modulation(i)
            self.set_layer(i)
            if stage == 1 and os.environ.get("KSUB", "") == "mod":
                self.dbg_dump(self.modraw[:].rearrange("p a b -> p (a b)"), 144, F32)
                break
            self.ffn(i, 0, 0)
            if stage < 2:
                break
            if i % 2 == 0:
                self.fourier(i)
            else:
                self.attention(i)
            if stage < 3:
                break
            self.ffn(i, 1, 2)


_CONST_CACHE = {}


def _consts():
    if _CONST_CACHE:
        return _CONST_CACHE
    c = {}
    c["ident"] = np.eye(128, dtype=np.float32)
    n = np.arange(128)
    ang = 2 * np.pi * np.outer(n, n) / 128.0
    c["ccsc"] = (np.concatenate([np.cos(ang), np.sin(ang)], 1) / np.sqrt(128.0)).astype(np.float32)
    t = np.arange(256)
    ang = 2 * np.pi * np.outer(t, t) / 256.0
    d = np.stack([np.cos(ang), -np.sin(ang)], 0) / 16.0
    d = d.reshape(2, 2, 128, 256).transpose(2, 1, 0, 3)
    c["d256"] = np.ascontiguousarray(d).astype(np.float32)
    J = np.zeros((64, 64), np.float32)
    J[np.arange(64), 63 - np.arange(64)] = 1.0
    c["j2"] = np.concatenate([J, J], 1)
    qc = np.arange(64)
    cs = np.clip(qc - 8, 0, 48)
    kc = np.arange(64)
    ok = (kc[:, None] >= cs[None, :]) & (kc[:, None] < cs[None, :] + 16)
    cm = np.where(ok, 0.0, NEG).astype(np.float32)
    cm = np.concatenate([cm, cm], 0)
    c["colmask"] = np.ascontiguousarray(cm).astype(np.float32)
    ab = np.zeros((8, 8, 64), np.float32)
    for a in range(8):
        ab[a, a, :] = 1.0
    c["augB"] = ab.reshape(8, 512)
    tt = np.arange(4096)
    for q in range(4):
        tp = 1024 * q + np.arange(1024)
        ang = 2 * np.pi * ((tt[:, None] * tp[None, :]) % 4096) / 4096.0
        dd = np.stack([np.cos(ang), -np.sin(ang)], 1) / 64.0
        c["dfts%d" % q] = np.ascontiguousarray(dd.reshape(32, 128, 2, 1024)).astype(ml_dtypes.bfloat16)
        A = np.zeros((8, 2, 16, 64), np.float32)
        for b in range(2):
            for jj in range(8):
                r = 16 * q + 8 * b + jj
                rs = min(max(r - 4, 0), 56)
                for xx in range(16):
                    kr = 16 * q - 4 + 8 * b + xx
                    valid = (rs <= kr < rs + 8)
                    A[jj, b, xx, :] = 0.0 if valid else NEG
        M = np.zeros((128, 2, 8, 8), np.float32)
        for b in range(2):
            for cc in range(8):
                for e2 in range(2):
                    for jj in range(8):
                        M[64 * e2:64 * e2 + 64, b, cc, jj] = 1.0 if A[jj, b, 2 * cc + e2, 0] == 0.0 else 0.0
        c["maskt%d" % q] = M
        et = np.arange(1536)
        row = np.clip(16 * q - 4 + et // 64, 0, 63)
        idx = row * 64 + et % 64
        c["gidx%d" % q] = np.ascontiguousarray(idx.reshape(12, 128).T).astype(np.int32)
    _CONST_CACHE.update(c)
    return c


def _fm(v):
    v = np.asarray(v, np.float32)
    lead = v.shape[:-1]
    r = v.reshape(lead + (8, 128))
    r = np.moveaxis(r, -1, 0)
    return np.ascontiguousarray(r)


_NC_CACHE = {}


def kernel(x_prompt, x_sample, cache_k, cache_v, c, c_ctx, w_mod, b_mod, norm_pre, norm_post,
           ffn_w1, ffn_w2, four_w_in, four_w_out, na_w_qkv, na_w_out, na_rpb, _n_layers=DEPTH):
    f = lambda a: np.ascontiguousarray(np.asarray(a, dtype=np.float32))
    x_prompt, x_sample, cache_k, cache_v = f(x_prompt), f(x_sample), f(cache_k), f(cache_v)
    c, c_ctx, w_mod, b_mod = f(c), f(c_ctx), f(w_mod), f(b_mod)
    norm_pre, norm_post = f(norm_pre), f(norm_post)
    ffn_w1, ffn_w2, four_w_in, four_w_out = f(ffn_w1), f(ffn_w2), f(four_w_in), f(four_w_out)
    na_w_qkv, na_w_out, na_rpb = f(na_w_qkv), f(na_w_out), f(na_rpb)
    K = _consts()
    if _n_layers not in _NC_CACHE:
        _NC_CACHE[_n_layers] = Builder(_n_layers).build()
    nc = _NC_CACHE[_n_layers]

    bmT = np.ascontiguousarray(b_mod.reshape(DEPTH, 72, 128).transpose(2, 0, 1))
    gpre = np.ascontiguousarray(norm_pre.reshape(DEPTH, 3, 8, 128).transpose(3, 0, 1, 2))
    gpost = np.ascontiguousarray(norm_post.reshape(DEPTH, 3, 8, 128).transpose(3, 0, 1, 2))
    rpbH = np.zeros((2, 16, 2, 22, 128), np.float32)
    for e in range(2):
        for k in range(22):
            d = 10 - k + e
            if -7 <= d <= 7:
                rpbH[:, :, e, k, 48:79] = na_rpb[:, :, d + 7, ::-1]
    rpbH = rpbH.reshape(-1)
    in_maps = []
    for cid in range(8):
        g, q = cid // 4, cid % 4
        xin = np.concatenate([x_prompt[2 * cid], x_prompt[2 * cid + 1],
                              x_sample[g, 1024 * q:1024 * q + 1024]], 0)
        condT = np.ascontiguousarray(np.stack([c_ctx, c[g]], 0).reshape(2, 8, 128).transpose(2, 1, 0))
        ck = cache_k[g]
        ckT = np.ascontiguousarray(ck.reshape(2, 8, 2, 256, 64).transpose(0, 2, 4, 1, 3).reshape(2, 128, 8, 256))
        cvv = cache_v[g]
        cv = np.ascontiguousarray(cvv.reshape(2, 16, 2, 128, 64).transpose(0, 3, 2, 1, 4).reshape(2, 128, 2, 1024))
        in_maps.append({
            "xin": np.ascontiguousarray(xin), "condT": condT, "w_mod": w_mod, "bmT": bmT, "gpre": gpre,
            "gpost": gpost, "ffn_w1": ffn_w1, "ffn_w2": ffn_w2, "four_w_in": four_w_in,
            "four_w_out": four_w_out, "na_w_qkv": na_w_qkv, "na_w_out": na_w_out, "ckT": ckT, "cv": cv,
            "rpbH": rpbH, "ident": K["ident"], "ccsc": K["ccsc"], "d256": K["d256"],
            "dfts": K["dfts%d" % q], "j2": K["j2"], "colmask": K["colmask"], "maskt": K["maskt%d" % q], "gidx": K["gidx%d" % q],
        })
    import os
    ncores = int(os.environ.get("KCORES", "8"))
    res = run_bass_kernel_spmd(nc, in_maps[:ncores], core_ids=list(range(ncores)))
    R = list(res.results)
    if ncores < 8:
        global _DBG_R
        _DBG_R = R
        R = R + [R[0]] * (8 - ncores)
    y_prompt = np.zeros((16, 256, D), np.float32)
    y_sample = np.zeros((2, 4096, D), np.float32)
    new_k = np.zeros((16, 2, 16, 256, 64), np.float32)
    new_v = np.zeros((16, 2, 16, 256, 64), np.float32)
    for cid in range(8):
        g, q = cid // 4, cid % 4
        yo = R[cid]["yout"]
        y_prompt[2 * cid] = yo[0:256]
        y_prompt[2 * cid + 1] = yo[256:512]
        y_sample[g, 1024 * q:1024 * q + 1024] = yo[512:1536]
        new_k[2 * cid:2 * cid + 2] = R[cid]["nk"]
        new_v[2 * cid:2 * cid + 2] = R[cid]["nv"]
    return (y_prompt, y_sample, new_k, new_v)
```

```python
import numpy as np
import concourse.bass as bass
import concourse.mybir as mybir
from concourse.bass_utils import run_bass_kernel_spmd

F32 = mybir.dt.float32
BF16 = mybir.dt.bfloat16
I32 = mybir.dt.int32
ALU = mybir.AluOpType
AF = mybir.ActivationFunctionType

_DSZ = {F32: 4, BF16: 2, I32: 4}


def _dsz(dt):
    try:
        return _DSZ[dt]
    except Exception:
        return mybir.dt.size(dt)


class _Op:
    __slots__ = ("idx", "eng", "method", "kwargs", "deps", "is_dma", "done", "waits",
                 "signals", "kad", "inc", "custom")


class Prog:
    ENGS = ("sync", "scalar", "vector", "gpsimd", "tensor")

    def __init__(self, nc, n_dma_sems=20, same_engine_sync=True):
        self.nc = nc
        self.ops = []
        self.recs = {}
        self.n_dma_sems = n_dma_sems
        self.dma_rr = {"h": 0, "g": 0}
        self.dma_last = {"h": [None] * n_dma_sems, "g": [None] * n_dma_sems}
        self.dma_cnt = {"h": [0] * n_dma_sems, "g": [0] * n_dma_sems}
        self.same_engine_sync = same_engine_sync
        self.psum_names = set()
        self.untracked = set()

    def region(self, ap):
        name = ap.name
        dims = ap.ap
        off = int(ap.offset)
        sp = str(ap.space)
        if sp in ("SB", "PSUM") or "SB" in sp or "PSUM" in sp:
            sz = _dsz(ap.dtype)
            pstep = dims[0][0]
            if name in self.psum_names:
                return (name, 0, 128, 0, 1 << 30)
            tshape = list(ap.tensor.shape)
            tsz = _dsz(ap.tensor.dtype)
            pstride = 1
            for s in tshape[1:]:
                pstride *= int(s)
            pstride = pstride * tsz // sz
            plo = off // pstride
            flo = off % pstride
            if pstep == 0:
                phi = plo + 1
            else:
                phi = plo + (dims[0][1] - 1) * (pstep // pstride) + 1
            fhi = flo + 1
            for st, c in dims[1:]:
                if st > 0:
                    fhi += (c - 1) * st
                elif st < 0:
                    flo += (c - 1) * st
            return (name, plo, phi, flo * sz, fhi * sz)
        else:
            sz = _dsz(ap.dtype)
            lo = off
            hi = off + 1
            for st, c in dims:
                if st > 0:
                    hi += (c - 1) * st
                elif st < 0:
                    lo += (c - 1) * st
            return (name, 0, 1, lo * sz, hi * sz)

    def add(self, eng, method, reads=(), writes=(), is_dma=False, custom=None, **kwargs):
        op = _Op()
        op.idx = len(self.ops)
        op.eng = eng
        op.method = method
        op.kwargs = kwargs
        op.is_dma = is_dma
        op.custom = custom
        op.deps = set()
        op.signals = False
        op.done = None
        op.waits = {}
        op.kad = None
        op.inc = 0
        rr = [self.region(a) for a in reads if a is not None]
        ww = [self.region(a) for a in writes if a is not None]
        rr = [r for r in rr if r[0] not in self.untracked]
        ww = [r for r in ww if r[0] not in self.untracked]
        ww = ww + [r for r in rr if r[0] in self.psum_names]
        rr = [r for r in rr if r[0] not in self.psum_names]
        for (name, plo, phi, flo, fhi) in rr:
            for rec in self.recs.get(name, ()):
                if rec[5] and rec[0] < phi and plo < rec[1] and rec[2] < fhi and flo < rec[3]:
                    op.deps.add(rec[4])
        for (name, plo, phi, flo, fhi) in ww:
            lst = self.recs.setdefault(name, [])
            keep = []
            for rec in lst:
                if rec[0] < phi and plo < rec[1] and rec[2] < fhi and flo < rec[3]:
                    op.deps.add(rec[4])
                    if plo <= rec[0] and rec[1] <= phi and flo <= rec[2] and rec[3] <= fhi:
                        continue
                keep.append(rec)
            keep.append([plo, phi, flo, fhi, op, True])
            self.recs[name] = keep
        for (name, plo, phi, flo, fhi) in rr:
            lst = self.recs.setdefault(name, [])
            rep = False
            if not is_dma:
                for rec in lst:
                    if (not rec[5]) and rec[0] == plo and rec[1] == phi and rec[2] == flo \
                            and rec[3] == fhi and rec[4].eng == eng and not rec[4].is_dma:
                        rec[4] = op
                        rep = True
                        break
            if not rep:
                lst.append([plo, phi, flo, fhi, op, False])
        op.deps.discard(op)
        if eng == "tensor":
            op.deps = {d for d in op.deps if d.eng != "tensor"}
        elif not self.same_engine_sync and not is_dma:
            op.deps = {d for d in op.deps if d.eng != eng or d.is_dma}
        if is_dma:
            pool = "g" if eng == "gpsimd" else "h"
            s = self.dma_rr[pool]
            self.dma_rr[pool] = (s + 1) % self.n_dma_sems
            prev = self.dma_last[pool][s]
            if prev is not None:
                op.deps.add(prev)
            self.dma_last[pool][s] = op
            self.dma_cnt[pool][s] += 16
            op.done = ("%s%d" % (pool, s), self.dma_cnt[pool][s])
            op.signals = True
            op.inc = 16
        for d in op.deps:
            d.signals = True
        self.ops.append(op)
        return op

    def _auto(self, eng, method, kw, extra_reads=(), extra_writes=()):
        reads = list(extra_reads)
        writes = list(extra_writes)
        for k, v in kw.items():
            if hasattr(v, "ap") and hasattr(v, "tensor"):
                if k in ("out", "accum_out", "ap"):
                    writes.append(v)
                else:
                    reads.append(v)
        return self.add(eng, method, reads=reads, writes=writes, **kw)

    def V(self, method, **kw):
        return self._auto("vector", method, kw)

    def G(self, method, **kw):
        return self._auto("gpsimd", method, kw)

    def A(self, method, **kw):
        return self._auto("scalar", method, kw)

    def mm(self, out, lhsT, rhs, start=True, stop=True, **kw):
        rd = [lhsT, rhs]
        return self.add("tensor", "matmul", reads=rd, writes=[out], out=out, lhsT=lhsT, rhs=rhs,
                        start=start, stop=stop, **kw)

    def transpose(self, out, in_, identity):
        return self.add("tensor", "transpose", reads=[in_, identity], writes=[out], out=out, in_=in_,
                        identity=identity)

    def dma(self, eng, out, in_, **kw):
        return self.add(eng, "dma_start", reads=[in_], writes=[out], is_dma=True, out=out, in_=in_, **kw)

    def emit(self):
        nc = self.nc
        cnt = {e: 0 for e in self.ENGS}
        for op in self.ops:
            if op.is_dma:
                continue
            if op.signals:
                cnt[op.eng] += 1
                op.done = ("e_" + op.eng, cnt[op.eng])
                op.inc = 1
        knows = {e: {} for e in self.ENGS}
        nwaits = 0
        for op in self.ops:
            kn = knows[op.eng]
            need = {}
            for p in sorted(op.deps, key=lambda o: o.idx):
                s, v = p.done
                if kn.get(s, 0) >= v:
                    continue
                if need.get(s, 0) < v:
                    need[s] = v
                kn[s] = v
                if p.kad is not None:
                    for s2, v2 in p.kad.items():
                        if kn.get(s2, 0) < v2:
                            kn[s2] = v2
            op.waits = need
            nwaits += len(need)
            if op.signals:
                kad = dict(kn)
                kad[op.done[0]] = op.done[1]
                op.kad = kad
        sem_names = ["e_" + e for e in self.ENGS] + ["h%d" % i for i in range(self.n_dma_sems)] + ["g%d" % i for i in range(self.n_dma_sems)]
        self.stats = dict(n_ops=len(self.ops), n_waits=nwaits,
                          per_eng={e: sum(1 for o in self.ops if o.eng == e) for e in self.ENGS})
        from contextlib import ExitStack
        with ExitStack() as st:
            sems = {n: st.enter_context(nc.semaphore(n)) for n in sem_names}
            block = st.enter_context(nc.Block())
            by_eng = {e: [o for o in self.ops if o.eng == e] for e in self.ENGS}

            def run(eng_obj, ops):
                for op in ops:
                    for s, v in op.waits.items():
                        eng_obj.wait_ge(sems[s], v)
                    if op.custom is not None:
                        ins = op.custom(eng_obj)
                    else:
                        ins = getattr(eng_obj, op.method)(**op.kwargs)
                    if op.signals and ins is not None:
                        ins.then_inc(sems[op.done[0]], op.inc)

            @block.sync
            def _(e):
                run(e, by_eng["sync"])

            @block.scalar
            def _(e):
                run(e, by_eng["scalar"])

            @block.vector
            def _(e):
                run(e, by_eng["vector"])

            @block.gpsimd
            def _(e):
                run(e, by_eng["gpsimd"])

            @block.tensor
            def _(e):
                run(e, by_eng["tensor"])


import ml_dtypes
from contextlib import ExitStack

D = 1024
DFF = 2816
NT = 1536
DEPTH = 4
EPS = 1e-6
NEG = -30000.0

WB_SZ = 11264
NWB = 4
MOD_PER_SLAB = [2, 2, 2, 2, 2, 2, 2, 1, 1, 1, 1]
HB_OFF = WB_SZ * NWB
HB_SZ = 24576
BIG_OFF = HB_OFF + HB_SZ
BIG_SZ = 69632
ARENA = BIG_OFF + BIG_SZ


def _tcs(tc):
    return slice(512 * tc, 512 * tc + 512)


class StopBuild(Exception):
    pass


class Builder:
    def __init__(self, n_layers=DEPTH):
        self.n_layers = n_layers
        nc = bass.Bass("TRN2", target_bir_lowering=False)
        self.nc = nc
        self.P = Prog(nc, n_dma_sems=16)
        self.st = ExitStack()

    def decl(self):
        nc = self.nc
        di = lambda n, s, d=F32: nc.dram_tensor(n, s, d, kind="ExternalInput")
        self.xin = di("xin", [NT, D])
        self.condT = di("condT", [128, 8, 2])
        self.w_mod = di("w_mod", [DEPTH, D, 9 * D])
        self.bmT = di("bmT", [128, DEPTH, 72])
        self.gpre = di("gpre", [128, DEPTH, 3, 8])
        self.gpost = di("gpost", [128, DEPTH, 3, 8])
        self.w1 = di("ffn_w1", [DEPTH, 2, D, 2 * DFF])
        self.w2 = di("ffn_w2", [DEPTH, 2, DFF, D])
        self.fwin = di("four_w_in", [2, D, D])
        self.fwout = di("four_w_out", [2, D, D])
        self.wqkv = di("na_w_qkv", [2, D, 3 * D])
        self.wout = di("na_w_out", [2, D, D])
        self.ckT = di("ckT", [2, 128, 8, 256])
        self.cv = di("cv", [2, 128, 2, 1024])
        self.rpbH = di("rpbH", [2 * 16 * 2 * 22 * 128])
        self.ident_d = di("ident", [128, 128])
        self.ccsc_d = di("ccsc", [128, 256])
        self.d256_d = di("d256", [128, 2, 2, 256])
        self.dfts_d = di("dfts", [32, 128, 2, 1024], BF16)
        self.j2_d = di("j2", [64, 128])
        self.colmask_d = di("colmask", [128, 64])
        self.maskt_d = di("maskt", [128, 2, 8, 8])
        self.gidx_d = di("gidx", [128, 12], I32)
        do = lambda n, s: nc.dram_tensor(n, s, F32, kind="ExternalOutput")
        self.yout = do("yout", [NT, D])
        self.nk = do("nk", [2, 2, 16, 256, 64])
        self.nv = do("nv", [2, 2, 16, 256, 64])
        self.ag_in = [nc.dram_tensor("ag_in0", [1024, 512], BF16)] * 2
        self.ag_out = [nc.dram_tensor("ag_out0", [4096, 512], BF16)] * 2
        self.vall = nc.dram_tensor("vall", [4096, 2048], BF16)

    def alloc(self):
        nc, st = self.nc, self.st
        sb = lambda n, s, d: st.enter_context(nc.sbuf_tensor(n, s, d))
        self.x = sb("x", [128, 8, NT], F32)
        self.arena = sb("arena", [128, ARENA // 2], BF16)
        self.ident = sb("identsb", [128, 128], F32)
        self.identb = sb("identb", [128, 128], BF16)
        self.ones = sb("ones", [128, 128], BF16)
        self.epsc = sb("epsc", [128, 1], F32)
        self.condsb = sb("condsb", [128, 8, 2], F32)
        self.scT = sb("scT", [128, 8, 2], BF16)
        self.bm = sb("bm", [128, DEPTH, 72], F32)
        self.gpre_s = sb("gpre_s", [128, DEPTH, 3, 8], F32)
        self.gpost_s = sb("gpost_s", [128, DEPTH, 3, 8], F32)
        self.modraw2 = [sb("modraw%d" % t, [128, 72, 2], F32) for t in range(2)]
        self.mA2 = [sb("mA%d" % t, [128, 3, 8, 2], F32) for t in range(2)]
        self.mG2 = [sb("mG%d" % t, [128, 3, 8, 2], F32) for t in range(2)]
        self.ccsc = sb("ccsc_s", [128, 256], BF16)
        self.d256 = sb("d256_s", [128, 2, 2, 256], BF16)
        self.j2 = sb("j2_s", [64, 128], F32)
        self.j2b = sb("j2b", [64, 128], BF16)
        self.maskt = sb("maskt_s", [128, 2, 8, 8], BF16)
        self.gidx = sb("gidx_s", [128, 12], I32)
        self.colmask_s = sb("colmask_s", [128, 64], F32)
        self.sq = [sb("sq%d" % i, [128, 512], BF16) for i in range(2)]
        self.rst = [sb("rst%d" % i, [128, 512], F32) for i in range(2)]
        self.tmp = [sb("tmp%d" % i, [128, 512], F32) for i in range(2)]
        self.ps = [st.enter_context(nc.psum_tensor("ps%d" % i, [128, 512], F32)) for i in range(8)]
        for p in self.ps:
            self.P.psum_names.add(p.name)
        self.cnt = {}

    def rot(self, key, n):
        v = self.cnt.get(key, 0)
        self.cnt[key] = v + 1
        return v % n

    def view(self, off, dt, *shape):
        n = 1
        for s in shape:
            n *= s
        assert off % 4 == 0
        if dt == BF16:
            ap = self.arena[:, off // 2: off // 2 + n]
        else:
            ap = self.arena[:, off // 2: off // 2 + 2 * n].bitcast(dt)
        if len(shape) == 2:
            ap = ap.rearrange("p (a b) -> p a b", a=shape[0])
        elif len(shape) == 3:
            ap = ap.rearrange("p (a b c) -> p a b c", a=shape[0], b=shape[1])
        return ap

    def wslot(self):
        i = self.rot("wb", NWB)
        return i * WB_SZ

    def ws_init(self, seq):
        self.ws_seq = seq
        self.ws_issued = 0
        self.ws_next = 0
        self.ws_slots = []
        self.gates = set()
        self.n_boot = 6

    def ws_issue_upto(self, n):
        while self.ws_issued < min(n, len(self.ws_seq)):
            kind, loads, gate = self.ws_seq[self.ws_issued]
            if gate is not None and gate not in self.gates:
                break
            if self.ws_issued < self.n_boot:
                off = (self.ws_issued % 14) * 8192
            else:
                off = (self.ws_issued % NWB) * WB_SZ
            self.ws_slots.append(off)
            for (eng, dstf, src) in loads:
                self.P.dma(eng, dstf(off), src)
            self.ws_issued += 1

    def ws_get(self, kind):
        i = self.ws_next
        assert self.ws_seq[i][0] == kind, (self.ws_seq[i][0], kind, i)
        if i < self.n_boot:
            self.ws_issue_upto(min(i + 14, self.n_boot))
        else:
            self.ws_issue_upto(i + NWB)
        assert self.ws_issued > i, (kind, i)
        self.ws_next += 1
        return self.ws_slots[i]

    def slab_views(self, off, nk, ncols):
        return self.view(off, BF16, nk, ncols)

    def build_ws_seq(self):
        seq = []
        V = self.view

        def wslab(kind, src_list, nk, ncols):
            loads = []
            for (src, c0) in src_list:
                c = src.shape[1]
                srcv = src.rearrange("(k p) n -> p k n", p=128)
                loads.append(("gpsimd",
                              (lambda off, c0=c0, c=c, nk=nk, ncols=ncols: V(off, BF16, nk, ncols)[:, :, c0:c0 + c]),
                              srcv))
            seq.append((kind, loads, None))

        def mod_slabs(i):
            wm = self.w_mod.ap()[i]
            for s in range(6):
                wslab("mod", [(wm[:, 512 * s:512 * s + 512], 0)], 8, 512)

        def ffn_slabs(i, a):
            w1 = self.w1.ap()[i, a]
            nxt = (i + 1 < self.n_layers) and not (a == 1 and i % 2 == 0)
            ms = 9 * a
            for sl in range(11):
                wslab("w1", [(w1[:, 256 * sl:256 * sl + 256], 0),
                             (w1[:, DFF + 256 * sl:DFF + 256 * sl + 256], 256)], 8, 512)
                if nxt:
                    for _ in range(1 if sl < 9 else 0):
                        wm = self.w_mod.ap()[i + 1]
                        wslab("mod", [(wm[:, 512 * ms:512 * ms + 512], 0)], 8, 512)
                        ms += 1
                if i == 0 and a == 0:
                    wm0 = self.w_mod.ap()[0]
                    for s0 in ([6 + sl] + ([17] if sl == 10 else [])):
                        wslab("mod", [(wm0[:, 512 * s0:512 * s0 + 512], 0)], 8, 512)
            w2 = self.w2.ap()[i, a]
            for s in range(4):
                wslab("w2", [(w2[:, 256 * s:256 * s + 256], 0)], 22, 256)

        def sq_slabs(kind, w, n, order=None):
            for s in (order if order is not None else range(n)):
                wslab(kind, [(w[:, 512 * s:512 * s + 512], 0)], 8, 512)

        for i in range(self.n_layers):
            if i == 0:
                mod_slabs(i)
            ffn_slabs(i, 0)
            j = i // 2
            if i % 2 == 0:
                sq_slabs("win", self.fwin.ap()[j], 2)
                if i + 1 < self.n_layers:
                    wm = self.w_mod.ap()[i + 1]
                    for ms in range(9, 18):
                        wslab("mod", [(wm[:, 512 * ms:512 * ms + 512], 0)], 8, 512)
                for ph in range(2):
                    for tt in range(32):
                        loads = [
                            ("scalar", (lambda off: V(off, BF16, 1024)),
                             self.vall.ap()[128 * tt:128 * tt + 128, 1024 * ph:1024 * ph + 1024]),
                            ("scalar", (lambda off: V(off + 2048, BF16, 2, 1024)),
                             self.dfts_d.ap()[tt]),
                        ]
                        seq.append(("dft", loads, ("ag", i, ph)))
                sq_slabs("wout", self.fwout.ap()[j], 2)
            else:
                sq_slabs("wqkv", self.wqkv.ap()[j], 6, order=(2, 3, 4, 5, 0, 1))
                sq_slabs("wout", self.wout.ap()[j], 2)
            ffn_slabs(i, 1)
        return seq

    def load_consts(self):
        P = self.P
        P.dma("sync", self.ident[:], self.ident_d.ap())
        P.V("tensor_copy", out=self.identb[:], in_=self.ident[:])
        P.V("memset", ap=self.ones[:], constant=1.0)
        P.V("memset", ap=self.epsc[:], constant=EPS)
        P.dma("sync", self.condsb[:], self.condT.ap())
        P.A("activation", out=self.scT[:], in_=self.condsb[:], func=AF.Silu)
        P.dma("sync", self.bm[:], self.bmT.ap())
        P.dma("sync", self.gpre_s[:], self.gpre.ap())
        P.dma("sync", self.gpost_s[:], self.gpost.ap())
        P.dma("gpsimd", self.ccsc[:], self.ccsc_d.ap())
        P.dma("gpsimd", self.d256[:], self.d256_d.ap())
        P.dma("sync", self.j2[:], self.j2_d.ap())
        P.dma("gpsimd", self.maskt[:], self.maskt_d.ap())
        P.dma("gpsimd", self.j2b[:], self.j2_d.ap())
        P.dma("sync", self.gidx[:], self.gidx_d.ap())
        P.dma("sync", self.colmask_s[:], self.colmask_d.ap())

    def load_x(self):
        P = self.P
        xin = self.xin.ap()
        for tt in range(12):
            so = ARENA - 8192 + (tt % 2) * 4096
            stg = self.view(so, F32, 1024)
            P.dma("sync", stg, xin[128 * tt:128 * tt + 128, :])
            for half in range(2):
                ps = self.ps[self.rot("ldps", 4)]
                for kk in range(4):
                    k = 4 * half + kk
                    P.transpose(ps[:, 128 * kk:128 * kk + 128], stg[:, 128 * k:128 * k + 128], self.ident[:])
                P.V("tensor_copy", out=self.x[:, 4 * half:4 * half + 4, 128 * tt:128 * tt + 128],
                    in_=ps[:].rearrange("p (a b) -> p a b", a=4))

    def store_x(self):
        P = self.P
        yo = self.yout.ap()
        outs = []
        for tt in range(12):
            so = BIG_OFF + (tt % 2) * 4096
            stg = self.view(so, F32, 1024)
            for half in range(2):
                ps = self.ps[self.rot("ldps", 4)]
                for kk in range(4):
                    k = 4 * half + kk
                    P.transpose(ps[:, 128 * kk:128 * kk + 128], self.x[:, k, 128 * tt:128 * tt + 128], self.ident[:])
                if half == 0:
                    P.V("tensor_copy", out=stg[:, 0:512], in_=ps[:])
                else:
                    P.A("activation", out=stg[:, 512:1024], in_=ps[:], func=AF.Copy)
            P.dma("sync", yo[128 * tt:128 * tt + 128, :], stg)
        self.out_aps.append(yo)

    def mod_step(self, i, s):
        P = self.P
        ps = self.ps[7]
        off = self.ws_get("mod")
        slab = self.view(off, BF16, 8, 512)
        for jj in range(4):
            for k in range(8):
                P.mm(ps[:, 2 * jj:2 * jj + 2], slab[:, k, 128 * jj:128 * jj + 128], self.scT[:, k, :],
                     start=(k == 0), stop=(k == 7), skip_group_check=True)
        modraw = self.modraw2[i % 2]
        P.V("tensor_tensor", out=modraw[:, 4 * s:4 * s + 4, :],
            in0=ps[:, 0:8].rearrange("p (j c) -> p j c", c=2),
            in1=self.bm[:, i, 4 * s:4 * s + 4].unsqueeze(2).broadcast_to([128, 4, 2]), op=ALU.add)

    def mod_finish(self, i, ss=(0, 1, 2)):
        P = self.P
        modraw, mA, mG = self.modraw2[i % 2], self.mA2[i % 2], self.mG2[i % 2]
        for s in ss:
            for c in range(2):
                P.V("scalar_tensor_tensor", out=mA[:, s, :, c],
                    in0=modraw[:, (3 * s + 1) * 8:(3 * s + 1) * 8 + 8, c], scalar=1.0,
                    in1=self.gpre_s[:, i, s, :], op0=ALU.add, op1=ALU.mult)
                P.V("scalar_tensor_tensor", out=mG[:, s, :, c],
                    in0=modraw[:, (3 * s + 2) * 8:(3 * s + 2) * 8 + 8, c],
                    scalar=(1.0 if s == 1 else 0.5),
                    in1=self.gpost_s[:, i, s, :], op0=ALU.mult, op1=ALU.mult)

    def set_layer(self, i):
        self.modraw, self.mA, self.mG = self.modraw2[i % 2], self.mA2[i % 2], self.mG2[i % 2]

    def modulation(self, i):
        for s in range(6):
            self.mod_step(i, s)
        self.mod_finish(i, ss=(0,))

    def mB(self, s, k, c):
        return self.modraw[:, (3 * s) * 8 + k, c:c + 1]

    def rstd_from(self, ps, r):
        P = self.P
        P.A("activation", out=r[:], in_=ps[:], func=AF.Ln, bias=self.epsc[:, 0:1], scale=1.0 / D)
        P.A("activation", out=r[:], in_=r[:], func=AF.Exp, scale=-0.5)

    def post_tc(self, pend, tc):
        P = self.P
        ybuf, ssb, s, mG = pend[:4]
        c = 0 if tc == 0 else 1
        r = self.rst[self.rot("rst", 2)]
        self.rstd_from(ssb[tc], r)
        for k in range(8):
            t = self.tmp[self.rot("tmp", 2)]
            P.V("scalar_tensor_tensor", out=t[:], in0=ybuf[:, k, _tcs(tc)], scalar=mG[:, s, k, c:c + 1],
                in1=r[:], op0=ALU.mult, op1=ALU.mult)
            P.V("tensor_tensor", out=self.x[:, k, _tcs(tc)], in0=self.x[:, k, _tcs(tc)], in1=t[:], op=ALU.add)

    def flush_post(self):
        pend = getattr(self, "pending_post", None)
        self.pending_post = None
        if pend is not None:
            for tc in pend[4]:
                self.post_tc(pend, tc)

    def prenorm(self, s):
        P = self.P
        h = self.view(HB_OFF, BF16, 8, NT)
        pend = getattr(self, "pending_post", None)
        self.pending_post = None
        for tc in range(3):
            if pend is not None and tc == 1:
                for t_ in pend[4]:
                    self.post_tc(pend, t_)
            c = 0 if tc == 0 else 1
            ps = self.ps[7]
            for k in range(8):
                sq = self.sq[self.rot("sq", 2)]
                P.A("activation", out=sq[:], in_=self.x[:, k, _tcs(tc)], func=AF.Square)
                P.mm(ps[:], self.ones[:], sq[:], start=(k == 0), stop=(k == 7))
            r = self.rst[self.rot("rst", 2)]
            self.rstd_from(ps, r)
            for k in range(8):
                t = self.tmp[self.rot("tmp", 2)]
                P.V("scalar_tensor_tensor", out=t[:], in0=self.x[:, k, _tcs(tc)], scalar=self.mA[:, s, k, c:c + 1],
                    in1=r[:], op0=ALU.mult, op1=ALU.mult)
                P.A("activation", out=h[:, k, _tcs(tc)], in_=t[:], func=AF.Identity, bias=self.mB(s, k, c), scale=1.0)
        return h

    def proj_fm(self, kind, nslab, ncol_chunks, nk, rhs_fn, ybuf, s):
        P = self.P
        ssb = [self.ps[4], self.ps[5], self.ps[6]]
        pend_ss = None
        for sl in range(nslab):
            off = self.ws_get(kind)
            slab = self.view(off, BF16, nk, 128 * ncol_chunks)
            for tc in range(3):
                for dd in range(ncol_chunks):
                    d = ncol_chunks * sl + dd
                    ps = self.ps[self.rot("py", 4)]
                    for k in range(nk):
                        P.mm(ps[:], slab[:, k, 128 * dd:128 * dd + 128], rhs_fn(k, tc), start=(k == 0), stop=(k == nk - 1))
                    if pend_ss is not None:
                        P.mm(*pend_ss[0], **pend_ss[1])
                    P.V("tensor_copy", out=ybuf[:, d, _tcs(tc)], in_=ps[:])
                    sq = self.sq[self.rot("sq", 2)]
                    P.A("activation", out=sq[:], in_=ps[:], func=AF.Square)
                    pend_ss = ((ssb[tc][:], self.ones[:], sq[:]), dict(start=(d == 0), stop=(d == 7), skip_group_check=True))
                if sl == nslab - 1:
                    P.mm(*pend_ss[0], **pend_ss[1])
                    pend_ss = None
                    if tc < 2:
                        self.post_tc((ybuf, ssb, s, self.mG), tc)
        self.pending_post = (ybuf, ssb, s, self.mG, (2,))

    def ffn(self, i, a, s):
        import os
        P = self.P
        h = self.prenorm(s)
        if os.environ.get("KSUB", "") == "pre":
            self.dbg_dump(h.rearrange("p a b -> p (a b)"), 8 * NT, BF16)
            raise StopBuild()
        gT = self.view(BIG_OFF, BF16, 22, NT)
        for sl in range(11):
            off = self.ws_get("w1")
            slab = self.view(off, BF16, 8, 512)
            for tc in range(3):
                for pp in range(2):
                    f = 2 * sl + pp
                    r = self.rot("pg", 3)
                    pg, pu = self.ps[2 * r], self.ps[2 * r + 1]
                    for k in range(8):
                        P.mm(pg[:], slab[:, k, 128 * pp:128 * pp + 128], h[:, k, _tcs(tc)], start=(k == 0), stop=(k == 7))
                    for k in range(8):
                        P.mm(pu[:], slab[:, k, 256 + 128 * pp:256 + 128 * pp + 128], h[:, k, _tcs(tc)],
                             start=(k == 0), stop=(k == 7))
                    t = self.tmp[self.rot("tmp", 2)]
                    P.A("activation", out=t[:], in_=pg[:], func=AF.Silu)
                    P.V("tensor_tensor", out=gT[:, f, _tcs(tc)], in0=t[:], in1=pu[:], op=ALU.mult)
            if i + 1 < self.n_layers and sl < 9 and not (a == 1 and i % 2 == 0):
                self.mod_step(i + 1, 9 * a + sl)
            if i == 0 and a == 0:
                self.mod_step(0, 6 + sl)
                if sl == 10:
                    self.mod_step(0, 17)
                    self.mod_finish(0, ss=(1, 2))
        if a == 1 and i + 1 < self.n_layers:
            self.mod_finish(i + 1)
        if os.environ.get("KSUB", "") == "w1":
            self.dbg_dump(gT.rearrange("p a b -> p (a b)"), 22 * NT, BF16)
            raise StopBuild()
        ybuf = h
        self.proj_fm("w2", 4, 2, 22, lambda k, tc: gT[:, k, _tcs(tc)], ybuf, s)

    def fourier(self, i):
        P = self.P
        j = i // 2
        h = self.prenorm(1)
        Vb = self.view(BIG_OFF, BF16, 12, 2048)
        UO = BIG_OFF + 49152
        pend_u = []

        def chan_dft(u, tc, g):
            for hh in range(2):
                pv = self.ps[4 + self.rot("pv", 2)]
                for tl in range(2):
                    t4 = 2 * hh + tl
                    P.mm(pv[:, 256 * tl:256 * tl + 256], u[:, 128 * t4:128 * t4 + 128], self.ccsc[:],
                         start=True, stop=True, skip_group_check=True)
                tt0 = 4 * tc + 2 * hh
                P.V("tensor_copy", out=Vb[:, tt0:tt0 + 2, 256 * g:256 * g + 256],
                    in_=pv[:].rearrange("p (a b) -> p a b", a=2))

        for sl in range(2):
            off = self.ws_get("win")
            slab = self.view(off, BF16, 8, 512)
            for gg in range(4):
                g = 4 * sl + gg
                for tc in range(3):
                    ps = self.ps[self.rot("py", 4)]
                    for k in range(8):
                        P.mm(ps[:], slab[:, k, 128 * gg:128 * gg + 128], h[:, k, _tcs(tc)], start=(k == 0), stop=(k == 7))
                    if pend_u:
                        chan_dft(*pend_u.pop(0))
                    u = self.view(UO + 1024 * self.rot("u", 4), BF16, 512)
                    P.A("activation", out=u, in_=ps[:], func=AF.Copy)
                    pend_u.append((u, tc, g))
                if g % 2 == 1:
                    while pend_u:
                        chan_dft(*pend_u.pop(0))
                    r = g // 2
                    P.dma("sync", self.ag_in[0].ap().rearrange("(a p) n -> p a n", p=128),
                          Vb[:, 4:12, 512 * r:512 * r + 512])
                    self.allgather(r, q="sync")
                    if r % 2 == 1:
                        self.gates.add(("ag", i, r // 2))
        fT = h
        if i + 1 < self.n_layers:
            for ms in range(9, 18):
                self.mod_step(i + 1, ms)
        for sq_ in range(2):
            for g in range(8):
                ps = self.ps[self.rot("py", 4)]
                n = 0
                for a in range(2):
                    for cs in range(2):
                        P.mm(ps[:, 0:256], Vb[:, 2 * sq_ + a, 256 * g + 128 * cs:256 * g + 128 * cs + 128],
                             self.d256[:, a, cs, :], start=(n == 0), stop=(n == 3))
                        n += 1
                P.V("tensor_copy", out=fT[:, g, 256 * sq_:256 * sq_ + 256], in_=ps[:, 0:256])
        for ph in range(2):
            for tt in range(32):
                off = self.ws_get("dft")
                vt = self.view(off, BF16, 1024)
                cst = self.view(off + 2048, BF16, 2, 1024)
                for gg in range(4):
                    for tq in range(2):
                        ps = self.ps[2 * gg + tq]
                        P.mm(ps[:], vt[:, 256 * gg:256 * gg + 128], cst[:, 0, 512 * tq:512 * tq + 512],
                             start=(tt == 0), stop=False)
                        P.mm(ps[:], vt[:, 256 * gg + 128:256 * gg + 256], cst[:, 1, 512 * tq:512 * tq + 512],
                             start=False, stop=(tt == 31))
            for gg in range(4):
                for tq in range(2):
                    ps = self.ps[2 * gg + tq]
                    o = fT[:, 4 * ph + gg, 512 + 512 * tq:512 + 512 * tq + 512]
                    if tq == 0:
                        P.V("tensor_copy", out=o, in_=ps[:])
                    else:
                        P.A("activation", out=o, in_=ps[:], func=AF.Copy)
        ybuf = self.view(BIG_OFF, BF16, 8, NT)
        self.proj_fm("wout", 2, 4, 8, lambda k, tc: fT[:, k, _tcs(tc)], ybuf, 1)

    def allgather(self, r, q="sync"):
        P = self.P
        ai, ao = self.ag_in[r % 2].ap(), self.ag_out[r % 2].ap()

        def cc(e):
            return e.collective_compute("AllGather", ALU.bypass, replica_groups=[[0, 1, 2, 3], [4, 5, 6, 7]],
                                        ins=[ai], outs=[ao])
        P.add("gpsimd", "cc", reads=[ai], writes=[ao], custom=cc)
        P.dma(q, self.vall.ap()[:, 512 * r:512 * r + 512], ao)

    def attention(self, i):
        P = self.P
        j = i // 2
        h = self.prenorm(1)
        B = BIG_OFF
        qT = self.view(B, BF16, 8, NT)
        Vext = self.view(B + 24576, BF16, 12, 1024)
        kTp = self.view(B + 49152, BF16, 8, 512)
        Vp = self.view(B + 57344, BF16, 4, 1024)
        kTc = self.view(B + 49152, BF16, 8, 256)
        Vc = self.view(B + 53248, BF16, 2, 1024)
        kText = self.view(HB_OFF, BF16, 8, NT)
        colmask = self.colmask_s[:].unsqueeze(1).broadcast_to([128, 22, 64])
        Ebuf = [self.sq[0][:], self.sq[1][:]]
        lg = [self.tmp[0][:], self.tmp[1][:]]
        rec = self.rst[0]
        stag32 = [self.view(B + 24576 + 2048 * t, F32, 512) for t in range(2)]
        stagb = [self.view(B + 24576 + 4096 + 1024 * t, BF16, 512) for t in range(2)]

        nk, nv = self.nk.ap(), self.nv.ap()
        for sl in (2, 3, 4, 5, 0, 1):
            off = self.ws_get("wqkv")
            slab = self.view(off, BF16, 8, 512)
            which = sl // 2
            half = sl % 2
            if which == 0 or which == 1:
                for tc in range(3 if which == 0 else 1):
                    for dd in range(4):
                        hp = 4 * half + dd
                        ps = self.ps[self.rot("py", 4)]
                        for k in range(8):
                            P.mm(ps[:], slab[:, k, 128 * dd:128 * dd + 128], h[:, k, _tcs(tc)], start=(k == 0), stop=(k == 7))
                        if which == 0:
                            P.A("activation", out=qT[:, hp, _tcs(tc)], in_=ps[:], func=AF.Copy, scale=0.125)
                        else:
                            P.V("tensor_copy", out=kTp[:, hp, 0:512], in_=ps[:])
            if which >= 1:
                dst = nk if which == 1 else nv
                for tt in range(12):
                    ps = self.ps[self.rot("py", 4)]
                    for k in range(8):
                        P.mm(ps[:], h[:, k, 128 * tt:128 * tt + 128], slab[:, k, :], start=(k == 0), stop=(k == 7))
                    if tt < 4:
                        s32 = stag32[self.rot("s32", 2)]
                        P.V("tensor_copy", out=s32, in_=ps[:])
                        sq_, t0 = tt // 2, 128 * (tt % 2)
                        d_ap = dst[sq_, j, 8 * half:8 * half + 8, t0:t0 + 128, :].rearrange("h t d -> t h d")
                        P.dma("sync", d_ap, s32.rearrange("p (h d) -> p h d", h=8))
                        if which == 2:
                            P.A("activation", out=Vp[:, tt, 512 * half:512 * half + 512], in_=ps[:], func=AF.Copy)
                    else:
                        sb_ = stagb[self.rot("sbf", 2)]
                        P.A("activation", out=sb_, in_=ps[:], func=AF.Copy)
                        P.dma("sync", self.ag_in[0].ap()[128 * (tt - 4):128 * (tt - 4) + 128, :], sb_)
                self.allgather((which - 1) * 2 + half)
        self.out_aps.append(nk)
        self.out_aps.append(nv)

        def attn_tail(pO, pD, rows, n, qslice):
            rc = self.rst[self.rot("rst", 2)][rows, 0:n]
            P.A("activation", out=rc, in_=pD[rows, 0:n], func=AF.Ln)
            P.A("activation", out=rc, in_=rc, func=AF.Exp, scale=-1.0)
            P.V("tensor_tensor", out=qslice, in0=pO[rows, 0:n], in1=rc, op=ALU.mult)

        tb0 = self.tmp[0][:].bitcast(BF16)
        Ep = [self.sq[0][:], self.sq[1][:], tb0[:, 0:512], tb0[:, 512:1024]]
        units = [(sq_, hp, e) for sq_ in range(2) for hp in range(8) for e in range(2)]

        def p_sgroup(u):
            sq_, hp, e = u
            rows = slice(64 * e, 64 * e + 64)
            qs = qT[rows, hp, 256 * sq_:256 * sq_ + 256]
            pS = self.ps[self.rot("pS", 4)]
            for kc in range(2):
                P.mm(pS[:, 256 * kc:256 * kc + 256], kTp[rows, hp, 256 * sq_ + 128 * kc:256 * sq_ + 128 * kc + 128], qs,
                     start=True, stop=True, skip_group_check=True)
            return pS

        pend = [p_sgroup(units[0]), p_sgroup(units[1])]
        for ui, u in enumerate(units):
            sq_, hp, e = u
            rows = slice(64 * e, 64 * e + 64)
            qs = qT[rows, hp, 256 * sq_:256 * sq_ + 256]
            if ui + 2 < len(units):
                pend.append(p_sgroup(units[ui + 2]))
            pS = pend[ui]
            r = self.rot("po", 2)
            pO, pD = self.ps[4 + 2 * r], self.ps[5 + 2 * r]
            E = Ep[self.rot("Ep", 4)]
            P.A("activation", out=E, in_=pS[:], func=AF.Exp)
            for kc in range(2):
                P.mm(pO[:, 0:256], Vp[:, 2 * sq_ + kc, 128 * hp:128 * hp + 128], E[:, 256 * kc:256 * kc + 256],
                     start=(kc == 0), stop=(kc == 1))
                P.mm(pD[:, 0:256], self.ones[:], E[:, 256 * kc:256 * kc + 256], start=(kc == 0), stop=(kc == 1))
            attn_tail(pO, pD, rows, 256, qs)

        ao = self.vall.ap()
        kvg0 = self.view(B + 49152, BF16, 2048)
        kvg1 = self.view(B + 53248, BF16, 2048)
        kvgs = [kvg0, kvg1]
        for et in range(12):
            kv = kvgs[et % 2]

            def gather(e, kv=kv, et=et):
                return e.indirect_dma_start(out=kv, out_offset=None, in_=ao,
                                            in_offset=bass.IndirectOffsetOnAxis(ap=self.gidx[:, et:et + 1], axis=0))
            P.add("gpsimd", "gather", reads=[ao, self.gidx[:, et:et + 1]], writes=[kv], custom=gather, is_dma=True)
            for half in range(2):
                ps = self.ps[self.rot("py", 4)]
                psb = ps[:].bitcast(BF16)
                for kk in range(4):
                    hp = 4 * half + kk
                    P.transpose(psb[:, 128 * kk:128 * kk + 128], kv[:, 128 * hp:128 * hp + 128], self.identb[:])
                P.V("tensor_copy", out=kText[:, 4 * half:4 * half + 4, 128 * et:128 * et + 128],
                    in_=psb[:, 0:512].rearrange("p (a b) -> p a b", a=4))
            P.A("activation", out=Vext[:, et, :], in_=kv[:, 1024:2048], func=AF.Copy)
        P.dma("gpsimd", kTc, self.ckT.ap()[j])
        P.dma("gpsimd", Vc, self.cv.ap()[j])

        T2rb = [self.view(B + 57344, BF16, 22, 64), self.view(B + 57344 + 2816, BF16, 22, 64)]
        Hkb = self.view(B + 57344 + 5632, BF16, 2, 16, 64)
        t0b = self.tmp[0][:].bitcast(BF16)
        t1b = self.tmp[1][:].bitcast(BF16)
        Ebuf = [self.sq[0][:], self.sq[1][:], t0b[:, 0:512], t0b[:, 512:1024], t1b[:, 0:512], t1b[:, 512:1024]]
        NE = len(Ebuf)
        P.V("memset", ap=T2rb[0], constant=0.0)
        P.V("memset", ap=T2rb[1], constant=0.0)
        cm8 = self.colmask_s[:].unsqueeze(1).broadcast_to([128, 8, 64])

        def build_steps(hd, T2r):
            for e2 in range(2):
                base = ((j * 16 + hd) * 2 + e2) * 22 * 128 + 3 * 128
                hap = bass.AP(self.rpbH, base, [[1, 64], [128, 16], [1, 64]])
                P.dma("gpsimd", Hkb[0:64, e2], hap)
            steps = []
            for e2 in range(2):
                for half in range(2):
                    def step(e2=e2, half=half, T2r=T2r):
                        ps = self.ps[self.rot("pS", 4)]
                        r2 = slice(64 * e2, 64 * e2 + 64)
                        P.mm(ps[:], self.j2b[:], Hkb[0:64, e2, 8 * half:8 * half + 8, :], start=True, stop=True)
                        P.V("tensor_tensor", out=T2r[r2, 3 + 8 * half:11 + 8 * half, :],
                            in0=ps[r2, :].rearrange("p (a b) -> p a b", a=8), in1=cm8[r2], op=ALU.add)
                    steps.append(step)
            return steps

        for st_ in build_steps(0, T2rb[0]):
            st_()
        for hd in range(16):
            hp, e = hd // 2, hd % 2
            rows = slice(64 * e, 64 * e + 64)
            T2r = T2rb[hd % 2]
            steps = build_steps(hd + 1, T2rb[(hd + 1) % 2]) if hd + 1 < 16 else []
            for b in range(2):
                QB = slice(512 + 512 * b, 1024 + 512 * b)
                qs = qT[rows, hp, QB]
                r = self.rot("po", 2)
                pO, pD = self.ps[4 + 2 * r], self.ps[5 + 2 * r]

                def sgroup(c, b=b, qs=qs, rows=rows, hp=hp, T2r=T2r):
                    pS = self.ps[self.rot("pS", 4)]
                    if c < 8:
                        ks = slice(512 * b + 128 * c, 512 * b + 128 * c + 128)
                        P.mm(pS[:], kText[rows, hp, ks], qs, start=True, stop=False)
                        P.mm(pS[:], self.identb[:], T2r[:, 14 - 2 * c:22 - 2 * c, :], start=False, stop=True)
                    else:
                        cc_ = c - 8
                        P.mm(pS[:], kTc[rows, hp, 128 * cc_:128 * cc_ + 128], qs, start=True, stop=True)
                    return pS

                pend = [sgroup(0), sgroup(1), sgroup(2)]
                for c in range(10):
                    if c + 3 < 10:
                        pend.append(sgroup(c + 3))
                    pS = pend[c]
                    E = Ebuf[self.rot("E", NE)]
                    P.A("activation", out=E, in_=pS[:], func=AF.Exp)
                    if c < 8:
                        E3 = E.rearrange("p (a b) -> p a b", a=8)
                        P.V("tensor_tensor", out=E3, in0=E3,
                            in1=self.maskt[:, b, c, :].unsqueeze(2).broadcast_to([128, 8, 64]), op=ALU.mult)
                    vsrc = Vext[:, 4 * b + c, 128 * hp:128 * hp + 128] if c < 8 else Vc[:, c - 8, 128 * hp:128 * hp + 128]
                    P.mm(pO[:], vsrc, E, start=(c == 0), stop=(c == 9))
                    P.mm(pD[:], self.ones[:], E, start=(c == 0), stop=(c == 9))
                for _ in range(2):
                    if steps:
                        steps.pop(0)()
                attn_tail(pO, pD, rows, 512, qs)
            while steps:
                steps.pop(0)()
        ybuf = self.view(B + 24576, BF16, 8, NT)
        self.proj_fm("wout", 2, 4, 8, lambda k, tc: qT[:, k, _tcs(tc)], ybuf, 1)

    def dbg_dump(self, ap, n, dt):
        d = self.nc.dram_tensor("dbg", [128, n], dt, kind="ExternalOutput")
        self.P.dma("sync", d.ap(), ap)
        self.out_aps.append(d.ap())

    def build(self):
        self.decl()
        self.alloc()
        self.out_aps = []
        self.ws_init(self.build_ws_seq())
        self.load_consts()
        self.load_x()
        import os
        stage = int(os.environ.get("KSTAGE", "9"))
        try:
            self.layers(stage)
        except StopBuild:
            pass
        self.flush_post()
        self.store_x()
        self.P.add("sync", "fence", reads=self.out_aps, custom=lambda e: None)
        self.P.emit()
        self.st.close()
        return self.nc

    def layers(self, stage):
        import os
        for i in range(self.n_layers):
            if stage < 1:
                break
            if i == 0:
                self.modulation(i)
            self.set_layer(i)
            if stage == 1 and os.environ.get("KSUB", "") == "mod":
                self.dbg_dump(self.modraw[:].rearrange("p a b -> p (a b)"), 144, F32)
                break
            self.ffn(i, 0, 0)
            if stage < 2:
                break
            if i % 2 == 0:
                self.fourier(i)
            else:
                self.attention(i)
            if stage < 3:
                break
            self.ffn(i, 1, 2)


_CONST_CACHE = {}


def _consts():
    if _CONST_CACHE:
        return _CONST_CACHE
    c = {}
    c["ident"] = np.eye(128, dtype=np.float32)
    n = np.arange(128)
    ang = 2 * np.pi * np.outer(n, n) / 128.0
    c["ccsc"] = (np.concatenate([np.cos(ang), np.sin(ang)], 1) / np.sqrt(128.0)).astype(np.float32)
    t = np.arange(256)
    ang = 2 * np.pi * np.outer(t, t) / 256.0
    d = np.stack([np.cos(ang), -np.sin(ang)], 0) / 16.0
    d = d.reshape(2, 2, 128, 256).transpose(2, 1, 0, 3)
    c["d256"] = np.ascontiguousarray(d).astype(np.float32)
    J = np.zeros((64, 64), np.float32)
    J[np.arange(64), 63 - np.arange(64)] = 1.0
    c["j2"] = np.concatenate([J, J], 1)
    qc = np.arange(64)
    cs = np.clip(qc - 8, 0, 48)
    kc = np.arange(64)
    ok = (kc[:, None] >= cs[None, :]) & (kc[:, None] < cs[None, :] + 16)
    cm = np.where(ok, 0.0, NEG).astype(np.float32)
    cm = np.concatenate([cm, cm], 0)
    c["colmask"] = np.ascontiguousarray(cm).astype(np.float32)
    ab = np.zeros((8, 8, 64), np.float32)
    for a in range(8):
        ab[a, a, :] = 1.0
    c["augB"] = ab.reshape(8, 512)
    tt = np.arange(4096)
    for q in range(4):
        tp = 1024 * q + np.arange(1024)
        ang = 2 * np.pi * ((tt[:, None] * tp[None, :]) % 4096) / 4096.0
        dd = np.stack([np.cos(ang), -np.sin(ang)], 1) / 64.0
        c["dfts%d" % q] = np.ascontiguousarray(dd.reshape(32, 128, 2, 1024)).astype(ml_dtypes.bfloat16)
        A = np.zeros((8, 2, 16, 64), np.float32)
        for b in range(2):
            for jj in range(8):
                r = 16 * q + 8 * b + jj
                rs = min(max(r - 4, 0), 56)
                for xx in range(16):
                    kr = 16 * q - 4 + 8 * b + xx
                    valid = (rs <= kr < rs + 8)
                    A[jj, b, xx, :] = 0.0 if valid else NEG
        M = np.zeros((128, 2, 8, 8), np.float32)
        for b in range(2):
            for cc in range(8):
                for e2 in range(2):
                    for jj in range(8):
                        M[64 * e2:64 * e2 + 64, b, cc, jj] = 1.0 if A[jj, b, 2 * cc + e2, 0] == 0.0 else 0.0
        c["maskt%d" % q] = M
        et = np.arange(1536)
        row = np.clip(16 * q - 4 + et // 64, 0, 63)
        idx = row * 64 + et % 64
        c["gidx%d" % q] = np.ascontiguousarray(idx.reshape(12, 128).T).astype(np.int32)
    _CONST_CACHE.update(c)
    return c


def _fm(v):
    v = np.asarray(v, np.float32)
    lead = v.shape[:-1]
    r = v.reshape(lead + (8, 128))
    r = np.moveaxis(r, -1, 0)
    return np.ascontiguousarray(r)


_NC_CACHE = {}


def kernel(x_prompt, x_sample, cache_k, cache_v, c, c_ctx, w_mod, b_mod, norm_pre, norm_post,
           ffn_w1, ffn_w2, four_w_in, four_w_out, na_w_qkv, na_w_out, na_rpb, _n_layers=DEPTH):
    f = lambda a: np.ascontiguousarray(np.asarray(a, dtype=np.float32))
    x_prompt, x_sample, cache_k, cache_v = f(x_prompt), f(x_sample), f(cache_k), f(cache_v)
    c, c_ctx, w_mod, b_mod = f(c), f(c_ctx), f(w_mod), f(b_mod)
    norm_pre, norm_post = f(norm_pre), f(norm_post)
    ffn_w1, ffn_w2, four_w_in, four_w_out = f(ffn_w1), f(ffn_w2), f(four_w_in), f(four_w_out)
    na_w_qkv, na_w_out, na_rpb = f(na_w_qkv), f(na_w_out), f(na_rpb)
    K = _consts()
    if _n_layers not in _NC_CACHE:
        _NC_CACHE[_n_layers] = Builder(_n_layers).build()
    nc = _NC_CACHE[_n_layers]

    bmT = np.ascontiguousarray(b_mod.reshape(DEPTH, 72, 128).transpose(2, 0, 1))
    gpre = np.ascontiguousarray(norm_pre.reshape(DEPTH, 3, 8, 128).transpose(3, 0, 1, 2))
    gpost = np.ascontiguousarray(norm_post.reshape(DEPTH, 3, 8, 128).transpose(3, 0, 1, 2))
    rpbH = np.zeros((2, 16, 2, 22, 128), np.float32)
    for e in range(2):
        for k in range(22):
            d = 10 - k + e
            if -7 <= d <= 7:
                rpbH[:, :, e, k, 48:79] = na_rpb[:, :, d + 7, ::-1]
    rpbH = rpbH.reshape(-1)
    in_maps = []
    for cid in range(8):
        g, q = cid // 4, cid % 4
        xin = np.concatenate([x_prompt[2 * cid], x_prompt[2 * cid + 1],
                              x_sample[g, 1024 * q:1024 * q + 1024]], 0)
        condT = np.ascontiguousarray(np.stack([c_ctx, c[g]], 0).reshape(2, 8, 128).transpose(2, 1, 0))
        ck = cache_k[g]
        ckT = np.ascontiguousarray(ck.reshape(2, 8, 2, 256, 64).transpose(0, 2, 4, 1, 3).reshape(2, 128, 8, 256))
        cvv = cache_v[g]
        cv = np.ascontiguousarray(cvv.reshape(2, 16, 2, 128, 64).transpose(0, 3, 2, 1, 4).reshape(2, 128, 2, 1024))
        in_maps.append({
            "xin": np.ascontiguousarray(xin), "condT": condT, "w_mod": w_mod, "bmT": bmT, "gpre": gpre,
            "gpost": gpost, "ffn_w1": ffn_w1, "ffn_w2": ffn_w2, "four_w_in": four_w_in,
            "four_w_out": four_w_out, "na_w_qkv": na_w_qkv, "na_w_out": na_w_out, "ckT": ckT, "cv": cv,
            "rpbH": rpbH, "ident": K["ident"], "ccsc": K["ccsc"], "d256": K["d256"],
            "dfts": K["dfts%d" % q], "j2": K["j2"], "colmask": K["colmask"], "maskt": K["maskt%d" % q], "gidx": K["gidx%d" % q],
        })
    import os
    ncores = int(os.environ.get("KCORES", "8"))
    res = run_bass_kernel_spmd(nc, in_maps[:ncores], core_ids=list(range(ncores)))
    R = list(res.results)
    if ncores < 8:
        global _DBG_R
        _DBG_R = R
        R = R + [R[0]] * (8 - ncores)
    y_prompt = np.zeros((16, 256, D), np.float32)
    y_sample = np.zeros((2, 4096, D), np.float32)
    new_k = np.zeros((16, 2, 16, 256, 64), np.float32)
    new_v = np.zeros((16, 2, 16, 256, 64), np.float32)
    for cid in range(8):
        g, q = cid // 4, cid % 4
        yo = R[cid]["yout"]
        y_prompt[2 * cid] = yo[0:256]
        y_prompt[2 * cid + 1] = yo[256:512]
        y_sample[g, 1024 * q:1024 * q + 1024] = yo[512:1536]
        new_k[2 * cid:2 * cid + 2] = R[cid]["nk"]
        new_v[2 * cid:2 * cid + 2] = R[cid]["nv"]
    return (y_prompt, y_sample, new_k, new_v)
```

```python
import numpy as np
import concourse.bass as bass
import concourse.mybir as mybir
from concourse.bass_utils import run_bass_kernel_spmd

F32 = mybir.dt.float32
BF16 = mybir.dt.bfloat16
I32 = mybir.dt.int32
ALU = mybir.AluOpType
AF = mybir.ActivationFunctionType

_DSZ = {F32: 4, BF16: 2, I32: 4}


def _dsz(dt):
    try:
        return _DSZ[dt]
    except Exception:
        return mybir.dt.size(dt)


class _Op:
    __slots__ = ("idx", "eng", "method", "kwargs", "deps", "is_dma", "done", "waits",
                 "signals", "kad", "inc", "custom")


class Prog:
    ENGS = ("sync", "scalar", "vector", "gpsimd", "tensor")

    def __init__(self, nc, n_dma_sems=20, same_engine_sync=True):
        self.nc = nc
        self.ops = []
        self.recs = {}
        self.n_dma_sems = n_dma_sems
        self.dma_rr = {"h": 0, "g": 0}
        self.dma_last = {"h": [None] * n_dma_sems, "g": [None] * n_dma_sems}
        self.dma_cnt = {"h": [0] * n_dma_sems, "g": [0] * n_dma_sems}
        self.same_engine_sync = same_engine_sync
        self.psum_names = set()
        self.untracked = set()

    def region(self, ap):
        name = ap.name
        dims = ap.ap
        off = int(ap.offset)
        sp = str(ap.space)
        if sp in ("SB", "PSUM") or "SB" in sp or "PSUM" in sp:
            sz = _dsz(ap.dtype)
            pstep = dims[0][0]
            if name in self.psum_names:
                return (name, 0, 128, 0, 1 << 30)
            tshape = list(ap.tensor.shape)
            tsz = _dsz(ap.tensor.dtype)
            pstride = 1
            for s in tshape[1:]:
                pstride *= int(s)
            pstride = pstride * tsz // sz
            plo = off // pstride
            flo = off % pstride
            if pstep == 0:
                phi = plo + 1
            else:
                phi = plo + (dims[0][1] - 1) * (pstep // pstride) + 1
            fhi = flo + 1
            for st, c in dims[1:]:
                if st > 0:
                    fhi += (c - 1) * st
                elif st < 0:
                    flo += (c - 1) * st
            return (name, plo, phi, flo * sz, fhi * sz)
        else:
            sz = _dsz(ap.dtype)
            lo = off
            hi = off + 1
            for st, c in dims:
                if st > 0:
                    hi += (c - 1) * st
                elif st < 0:
                    lo += (c - 1) * st
            return (name, 0, 1, lo * sz, hi * sz)

    def add(self, eng, method, reads=(), writes=(), is_dma=False, custom=None, **kwargs):
        op = _Op()
        op.idx = len(self.ops)
        op.eng = eng
        op.method = method
        op.kwargs = kwargs
        op.is_dma = is_dma
        op.custom = custom
        op.deps = set()
        op.signals = False
        op.done = None
        op.waits = {}
        op.kad = None
        op.inc = 0
        rr = [self.region(a) for a in reads if a is not None]
        ww = [self.region(a) for a in writes if a is not None]
        rr = [r for r in rr if r[0] not in self.untracked]
        ww = [r for r in ww if r[0] not in self.untracked]
        ww = ww + [r for r in rr if r[0] in self.psum_names]
        rr = [r for r in rr if r[0] not in self.psum_names]
        for (name, plo, phi, flo, fhi) in rr:
            for rec in self.recs.get(name, ()):
                if rec[5] and rec[0] < phi and plo < rec[1] and rec[2] < fhi and flo < rec[3]:
                    op.deps.add(rec[4])
        for (name, plo, phi, flo, fhi) in ww:
            lst = self.recs.setdefault(name, [])
            keep = []
            for rec in lst:
                if rec[0] < phi and plo < rec[1] and rec[2] < fhi and flo < rec[3]:
                    op.deps.add(rec[4])
                    if plo <= rec[0] and rec[1] <= phi and flo <= rec[2] and rec[3] <= fhi:
                        continue
                keep.append(rec)
            keep.append([plo, phi, flo, fhi, op, True])
            self.recs[name] = keep
        for (name, plo, phi, flo, fhi) in rr:
            lst = self.recs.setdefault(name, [])
            rep = False
            if not is_dma:
                for rec in lst:
                    if (not rec[5]) and rec[0] == plo and rec[1] == phi and rec[2] == flo \
                            and rec[3] == fhi and rec[4].eng == eng and not rec[4].is_dma:
                        rec[4] = op
                        rep = True
                        break
            if not rep:
                lst.append([plo, phi, flo, fhi, op, False])
        op.deps.discard(op)
        if eng == "tensor":
            op.deps = {d for d in op.deps if d.eng != "tensor"}
        elif not self.same_engine_sync and not is_dma:
            op.deps = {d for d in op.deps if d.eng != eng or d.is_dma}
        if is_dma:
            pool = "g" if eng == "gpsimd" else "h"
            s = self.dma_rr[pool]
            self.dma_rr[pool] = (s + 1) % self.n_dma_sems
            prev = self.dma_last[pool][s]
            if prev is not None:
                op.deps.add(prev)
            self.dma_last[pool][s] = op
            self.dma_cnt[pool][s] += 16
            op.done = ("%s%d" % (pool, s), self.dma_cnt[pool][s])
            op.signals = True
            op.inc = 16
        for d in op.deps:
            d.signals = True
        self.ops.append(op)
        return op

    def _auto(self, eng, method, kw, extra_reads=(), extra_writes=()):
        reads = list(extra_reads)
        writes = list(extra_writes)
        for k, v in kw.items():
            if hasattr(v, "ap") and hasattr(v, "tensor"):
                if k in ("out", "accum_out", "ap"):
                    writes.append(v)
                else:
                    reads.append(v)
        return self.add(eng, method, reads=reads, writes=writes, **kw)

    def V(self, method, **kw):
        return self._auto("vector", method, kw)

    def G(self, method, **kw):
        return self._auto("gpsimd", method, kw)

    def A(self, method, **kw):
        return self._auto("scalar", method, kw)

    def mm(self, out, lhsT, rhs, start=True, stop=True, **kw):
        rd = [lhsT, rhs]
        return self.add("tensor", "matmul", reads=rd, writes=[out], out=out, lhsT=lhsT, rhs=rhs,
                        start=start, stop=stop, **kw)

    def transpose(self, out, in_, identity):
        return self.add("tensor", "transpose", reads=[in_, identity], writes=[out], out=out, in_=in_,
                        identity=identity)

    def dma(self, eng, out, in_, **kw):
        return self.add(eng, "dma_start", reads=[in_], writes=[out], is_dma=True, out=out, in_=in_, **kw)

    def emit(self):
        nc = self.nc
        cnt = {e: 0 for e in self.ENGS}
        for op in self.ops:
            if op.is_dma:
                continue
            if op.signals:
                cnt[op.eng] += 1
                op.done = ("e_" + op.eng, cnt[op.eng])
                op.inc = 1
        knows = {e: {} for e in self.ENGS}
        nwaits = 0
        for op in self.ops:
            kn = knows[op.eng]
            need = {}
            for p in sorted(op.deps, key=lambda o: o.idx):
                s, v = p.done
                if kn.get(s, 0) >= v:
                    continue
                if need.get(s, 0) < v:
                    need[s] = v
                kn[s] = v
                if p.kad is not None:
                    for s2, v2 in p.kad.items():
                        if kn.get(s2, 0) < v2:
                            kn[s2] = v2
            op.waits = need
            nwaits += len(need)
            if op.signals:
                kad = dict(kn)
                kad[op.done[0]] = op.done[1]
                op.kad = kad
        sem_names = ["e_" + e for e in self.ENGS] + ["h%d" % i for i in range(self.n_dma_sems)] + ["g%d" % i for i in range(self.n_dma_sems)]
        self.stats = dict(n_ops=len(self.ops), n_waits=nwaits,
                          per_eng={e: sum(1 for o in self.ops if o.eng == e) for e in self.ENGS})
        from contextlib import ExitStack
        with ExitStack() as st:
            sems = {n: st.enter_context(nc.semaphore(n)) for n in sem_names}
            block = st.enter_context(nc.Block())
            by_eng = {e: [o for o in self.ops if o.eng == e] for e in self.ENGS}

            def run(eng_obj, ops):
                for op in ops:
                    for s, v in op.waits.items():
                        eng_obj.wait_ge(sems[s], v)
                    if op.custom is not None:
                        ins = op.custom(eng_obj)
                    else:
                        ins = getattr(eng_obj, op.method)(**op.kwargs)
                    if op.signals and ins is not None:
                        ins.then_inc(sems[op.done[0]], op.inc)

            @block.sync
            def _(e):
                run(e, by_eng["sync"])

            @block.scalar
            def _(e):
                run(e, by_eng["scalar"])

            @block.vector
            def _(e):
                run(e, by_eng["vector"])

            @block.gpsimd
            def _(e):
                run(e, by_eng["gpsimd"])

            @block.tensor
            def _(e):
                run(e, by_eng["tensor"])


import ml_dtypes
from contextlib import ExitStack

D = 1024
DFF = 2816
NT = 1536
DEPTH = 4
EPS = 1e-6
NEG = -30000.0

WB_SZ = 11264
NWB = 4
MOD_PER_SLAB = [2, 2, 2, 2, 2, 2, 2, 1, 1, 1, 1]
HB_OFF = WB_SZ * NWB
HB_SZ = 24576
BIG_OFF = HB_OFF + HB_SZ
BIG_SZ = 69632
ARENA = BIG_OFF + BIG_SZ


def _tcs(tc):
    return slice(512 * tc, 512 * tc + 512)


class StopBuild(Exception):
    pass


class Builder:
    def __init__(self, n_layers=DEPTH):
        self.n_layers = n_layers
        nc = bass.Bass("TRN2", target_bir_lowering=False)
        self.nc = nc
        self.P = Prog(nc, n_dma_sems=16)
        self.st = ExitStack()

    def decl(self):
        nc = self.nc
        di = lambda n, s, d=F32: nc.dram_tensor(n, s, d, kind="ExternalInput")
        self.xin = di("xin", [NT, D])
        self.condT = di("condT", [128, 8, 2])
        self.w_mod = di("w_mod", [DEPTH, D, 9 * D])
        self.bmT = di("bmT", [128, DEPTH, 72])
        self.gpre = di("gpre", [128, DEPTH, 3, 8])
        self.gpost = di("gpost", [128, DEPTH, 3, 8])
        self.w1 = di("ffn_w1", [DEPTH, 2, D, 2 * DFF])
        self.w2 = di("ffn_w2", [DEPTH, 2, DFF, D])
        self.fwin = di("four_w_in", [2, D, D])
        self.fwout = di("four_w_out", [2, D, D])
        self.wqkv = di("na_w_qkv", [2, D, 3 * D])
        self.wout = di("na_w_out", [2, D, D])
        self.ckT = di("ckT", [2, 128, 8, 256])
        self.cv = di("cv", [2, 128, 2, 1024])
        self.rpbH = di("rpbH", [2 * 16 * 2 * 22 * 128])
        self.ident_d = di("ident", [128, 128])
        self.ccsc_d = di("ccsc", [128, 256])
        self.d256_d = di("d256", [128, 2, 2, 256])
        self.dfts_d = di("dfts", [32, 128, 2, 1024], BF16)
        self.j2_d = di("j2", [64, 128])
        self.colmask_d = di("colmask", [128, 64])
        self.maskt_d = di("maskt", [128, 2, 8, 8])
        self.gidx_d = di("gidx", [128, 12], I32)
        do = lambda n, s: nc.dram_tensor(n, s, F32, kind="ExternalOutput")
        self.yout = do("yout", [NT, D])
        self.nk = do("nk", [2, 2, 16, 256, 64])
        self.nv = do("nv", [2, 2, 16, 256, 64])
        self.ag_in = [nc.dram_tensor("ag_in0", [1024, 512], BF16)] * 2
        self.ag_out = [nc.dram_tensor("ag_out0", [4096, 512], BF16)] * 2
        self.vall = nc.dram_tensor("vall", [4096, 2048], BF16)

    def alloc(self):
        nc, st = self.nc, self.st
        sb = lambda n, s, d: st.enter_context(nc.sbuf_tensor(n, s, d))
        self.x = sb("x", [128, 8, NT], F32)
        self.arena = sb("arena", [128, ARENA // 2], BF16)
        self.ident = sb("identsb", [128, 128], F32)
        self.identb = sb("identb", [128, 128], BF16)
        self.ones = sb("ones", [128, 128], BF16)
        self.epsc = sb("epsc", [128, 1], F32)
        self.condsb = sb("condsb", [128, 8, 2], F32)
        self.scT = sb("scT", [128, 8, 2], BF16)
        self.bm = sb("bm", [128, DEPTH, 72], F32)
        self.gpre_s = sb("gpre_s", [128, DEPTH, 3, 8], F32)
        self.gpost_s = sb("gpost_s", [128, DEPTH, 3, 8], F32)
        self.modraw2 = [sb("modraw%d" % t, [128, 72, 2], F32) for t in range(2)]
        self.mA2 = [sb("mA%d" % t, [128, 3, 8, 2], F32) for t in range(2)]
        self.mG2 = [sb("mG%d" % t, [128, 3, 8, 2], F32) for t in range(2)]
        self.ccsc = sb("ccsc_s", [128, 256], BF16)
        self.d256 = sb("d256_s", [128, 2, 2, 256], BF16)
        self.j2 = sb("j2_s", [64, 128], F32)
        self.j2b = sb("j2b", [64, 128], BF16)
        self.maskt = sb("maskt_s", [128, 2, 8, 8], BF16)
        self.gidx = sb("gidx_s", [128, 12], I32)
        self.colmask_s = sb("colmask_s", [128, 64], F32)
        self.sq = [sb("sq%d" % i, [128, 512], BF16) for i in range(2)]
        self.rst = [sb("rst%d" % i, [128, 512], F32) for i in range(2)]
        self.tmp = [sb("tmp%d" % i, [128, 512], F32) for i in range(2)]
        self.ps = [st.enter_context(nc.psum_tensor("ps%d" % i, [128, 512], F32)) for i in range(8)]
        for p in self.ps:
            self.P.psum_names.add(p.name)
        self.cnt = {}

    def rot(self, key, n):
        v = self.cnt.get(key, 0)
        self.cnt[key] = v + 1
        return v % n

    def view(self, off, dt, *shape):
        n = 1
        for s in shape:
            n *= s
        assert off % 4 == 0
        if dt == BF16:
            ap = self.arena[:, off // 2: off // 2 + n]
        else:
            ap = self.arena[:, off // 2: off // 2 + 2 * n].bitcast(dt)
        if len(shape) == 2:
            ap = ap.rearrange("p (a b) -> p a b", a=shape[0])
        elif len(shape) == 3:
            ap = ap.rearrange("p (a b c) -> p a b c", a=shape[0], b=shape[1])
        return ap

    def wslot(self):
        i = self.rot("wb", NWB)
        return i * WB_SZ

    def ws_init(self, seq):
        self.ws_seq = seq
        self.ws_issued = 0
        self.ws_next = 0
        self.ws_slots = []
        self.gates = set()
        self.n_boot = 6

    def ws_issue_upto(self, n):
        while self.ws_issued < min(n, len(self.ws_seq)):
            kind, loads, gate = self.ws_seq[self.ws_issued]
            if gate is not None and gate not in self.gates:
                break
            if self.ws_issued < self.n_boot:
                off = (self.ws_issued % 14) * 8192
            else:
                off = (self.ws_issued % NWB) * WB_SZ
            self.ws_slots.append(off)
            for (eng, dstf, src) in loads:
                self.P.dma(eng, dstf(off), src)
            self.ws_issued += 1

    def ws_get(self, kind):
        i = self.ws_next
        assert self.ws_seq[i][0] == kind, (self.ws_seq[i][0], kind, i)
        if i < self.n_boot:
            self.ws_issue_upto(min(i + 14, self.n_boot))
        else:
            self.ws_issue_upto(i + NWB)
        assert self.ws_issued > i, (kind, i)
        self.ws_next += 1
        return self.ws_slots[i]

    def slab_views(self, off, nk, ncols):
        return self.view(off, BF16, nk, ncols)

    def build_ws_seq(self):
        seq = []
        V = self.view

        def wslab(kind, src_list, nk, ncols):
            loads = []
            for (src, c0) in src_list:
                c = src.shape[1]
                srcv = src.rearrange("(k p) n -> p k n", p=128)
                loads.append(("gpsimd",
                              (lambda off, c0=c0, c=c, nk=nk, ncols=ncols: V(off, BF16, nk, ncols)[:, :, c0:c0 + c]),
                              srcv))
            seq.append((kind, loads, None))

        def mod_slabs(i):
            wm = self.w_mod.ap()[i]
            for s in range(6):
                wslab("mod", [(wm[:, 512 * s:512 * s + 512], 0)], 8, 512)

        def ffn_slabs(i, a):
            w1 = self.w1.ap()[i, a]
            nxt = (i + 1 < self.n_layers) and not (a == 1 and i % 2 == 0)
            ms = 9 * a
            for sl in range(11):
                wslab("w1", [(w1[:, 256 * sl:256 * sl + 256], 0),
                             (w1[:, DFF + 256 * sl:DFF + 256 * sl + 256], 256)], 8, 512)
                if nxt:
                    for _ in range(1 if sl < 9 else 0):
                        wm = self.w_mod.ap()[i + 1]
                        wslab("mod", [(wm[:, 512 * ms:512 * ms + 512], 0)], 8, 512)
                        ms += 1
                if i == 0 and a == 0:
                    wm0 = self.w_mod.ap()[0]
                    for s0 in ([6 + sl] + ([17] if sl == 10 else [])):
                        wslab("mod", [(wm0[:, 512 * s0:512 * s0 + 512], 0)], 8, 512)
            w2 = self.w2.ap()[i, a]
            for s in range(4):
                wslab("w2", [(w2[:, 256 * s:256 * s + 256], 0)], 22, 256)

        def sq_slabs(kind, w, n, order=None):
            for s in (order if order is not None else range(n)):
                wslab(kind, [(w[:, 512 * s:512 * s + 512], 0)], 8, 512)

        for i in range(self.n_layers):
            if i == 0:
                mod_slabs(i)
            ffn_slabs(i, 0)
            j = i // 2
            if i % 2 == 0:
                sq_slabs("win", self.fwin.ap()[j], 2)
                if i + 1 < self.n_layers:
                    wm = self.w_mod.ap()[i + 1]
                    for ms in range(9, 18):
                        wslab("mod", [(wm[:, 512 * ms:512 * ms + 512], 0)], 8, 512)
                for ph in range(2):
                    for tt in range(32):
                        loads = [
                            ("scalar", (lambda off: V(off, BF16, 1024)),
                             self.vall.ap()[128 * tt:128 * tt + 128, 1024 * ph:1024 * ph + 1024]),
                            ("scalar", (lambda off: V(off + 2048, BF16, 2, 1024)),
                             self.dfts_d.ap()[tt]),
                        ]
                        seq.append(("dft", loads, ("ag", i, ph)))
                sq_slabs("wout", self.fwout.ap()[j], 2)
            else:
                sq_slabs("wqkv", self.wqkv.ap()[j], 6, order=(2, 3, 4, 5, 0, 1))
                sq_slabs("wout", self.wout.ap()[j], 2)
            ffn_slabs(i, 1)
        return seq

    def load_consts(self):
        P = self.P
        P.dma("sync", self.ident[:], self.ident_d.ap())
        P.V("tensor_copy", out=self.identb[:], in_=self.ident[:])
        P.V("memset", ap=self.ones[:], constant=1.0)
        P.V("memset", ap=self.epsc[:], constant=EPS)
        P.dma("sync", self.condsb[:], self.condT.ap())
        P.A("activation", out=self.scT[:], in_=self.condsb[:], func=AF.Silu)
        P.dma("sync", self.bm[:], self.bmT.ap())
        P.dma("sync", self.gpre_s[:], self.gpre.ap())
        P.dma("sync", self.gpost_s[:], self.gpost.ap())
        P.dma("gpsimd", self.ccsc[:], self.ccsc_d.ap())
        P.dma("gpsimd", self.d256[:], self.d256_d.ap())
        P.dma("sync", self.j2[:], self.j2_d.ap())
        P.dma("gpsimd", self.maskt[:], self.maskt_d.ap())
        P.dma("gpsimd", self.j2b[:], self.j2_d.ap())
        P.dma("sync", self.gidx[:], self.gidx_d.ap())
        P.dma("sync", self.colmask_s[:], self.colmask_d.ap())

    def load_x(self):
        P = self.P
        xin = self.xin.ap()
        for tt in range(12):
            so = ARENA - 8192 + (tt % 2) * 4096
            stg = self.view(so, F32, 1024)
            P.dma("sync", stg, xin[128 * tt:128 * tt + 128, :])
            for half in range(2):
                ps = self.ps[self.rot("ldps", 4)]
                for kk in range(4):
                    k = 4 * half + kk
                    P.transpose(ps[:, 128 * kk:128 * kk + 128], stg[:, 128 * k:128 * k + 128], self.ident[:])
                P.V("tensor_copy", out=self.x[:, 4 * half:4 * half + 4, 128 * tt:128 * tt + 128],
                    in_=ps[:].rearrange("p (a b) -> p a b", a=4))

    def store_x(self):
        P = self.P
        yo = self.yout.ap()
        outs = []
        for tt in range(12):
            so = BIG_OFF + (tt % 2) * 4096
            stg = self.view(so, F32, 1024)
            for half in range(2):
                ps = self.ps[self.rot("ldps", 4)]
                for kk in range(4):
                    k = 4 * half + kk
                    P.transpose(ps[:, 128 * kk:128 * kk + 128], self.x[:, k, 128 * tt:128 * tt + 128], self.ident[:])
                if half == 0:
                    P.V("tensor_copy", out=stg[:, 0:512], in_=ps[:])
                else:
                    P.A("activation", out=stg[:, 512:1024], in_=ps[:], func=AF.Copy)
            P.dma("sync", yo[128 * tt:128 * tt + 128, :], stg)
        self.out_aps.append(yo)

    def mod_step(self, i, s):
        P = self.P
        ps = self.ps[7]
        off = self.ws_get("mod")
        slab = self.view(off, BF16, 8, 512)
        for jj in range(4):
            for k in range(8):
                P.mm(ps[:, 2 * jj:2 * jj + 2], slab[:, k, 128 * jj:128 * jj + 128], self.scT[:, k, :],
                     start=(k == 0), stop=(k == 7), skip_group_check=True)
        modraw = self.modraw2[i % 2]
        P.V("tensor_tensor", out=modraw[:, 4 * s:4 * s + 4, :],
            in0=ps[:, 0:8].rearrange("p (j c) -> p j c", c=2),
            in1=self.bm[:, i, 4 * s:4 * s + 4].unsqueeze(2).broadcast_to([128, 4, 2]), op=ALU.add)

    def mod_finish(self, i, ss=(0, 1, 2)):
        P = self.P
        modraw, mA, mG = self.modraw2[i % 2], self.mA2[i % 2], self.mG2[i % 2]
        for s in ss:
            for c in range(2):
                P.V("scalar_tensor_tensor", out=mA[:, s, :, c],
                    in0=modraw[:, (3 * s + 1) * 8:(3 * s + 1) * 8 + 8, c], scalar=1.0,
                    in1=self.gpre_s[:, i, s, :], op0=ALU.add, op1=ALU.mult)
                P.V("scalar_tensor_tensor", out=mG[:, s, :, c],
                    in0=modraw[:, (3 * s + 2) * 8:(3 * s + 2) * 8 + 8, c],
                    scalar=(1.0 if s == 1 else 0.5),
                    in1=self.gpost_s[:, i, s, :], op0=ALU.mult, op1=ALU.mult)

    def set_layer(self, i):
        self.modraw, self.mA, self.mG = self.modraw2[i % 2], self.mA2[i % 2], self.mG2[i % 2]

    def modulation(self, i):
        for s in range(6):
            self.mod_step(i, s)
        self.mod_finish(i, ss=(0,))

    def mB(self, s, k, c):
        return self.modraw[:, (3 * s) * 8 + k, c:c + 1]

    def rstd_from(self, ps, r):
        P = self.P
        P.A("activation", out=r[:], in_=ps[:], func=AF.Ln, bias=self.epsc[:, 0:1], scale=1.0 / D)
        P.A("activation", out=r[:], in_=r[:], func=AF.Exp, scale=-0.5)

    def post_tc(self, pend, tc):
        P = self.P
        ybuf, ssb, s, mG = pend[:4]
        c = 0 if tc == 0 else 1
        r = self.rst[self.rot("rst", 2)]
        self.rstd_from(ssb[tc], r)
        for k in range(8):
            t = self.tmp[self.rot("tmp", 2)]
            P.V("scalar_tensor_tensor", out=t[:], in0=ybuf[:, k, _tcs(tc)], scalar=mG[:, s, k, c:c + 1],
                in1=r[:], op0=ALU.mult, op1=ALU.mult)
            P.V("tensor_tensor", out=self.x[:, k, _tcs(tc)], in0=self.x[:, k, _tcs(tc)], in1=t[:], op=ALU.add)

    def flush_post(self):
        pend = getattr(self, "pending_post", None)
        self.pending_post = None
        if pend is not None:
            for tc in pend[4]:
                self.post_tc(pend, tc)

    def prenorm(self, s):
        P = self.P
        h = self.view(HB_OFF, BF16, 8, NT)
        pend = getattr(self, "pending_post", None)
        self.pending_post = None
        for tc in range(3):
            if pend is not None and tc == 1:
                for t_ in pend[4]:
                    self.post_tc(pend, t_)
            c = 0 if tc == 0 else 1
            ps = self.ps[7]
            for k in range(8):
                sq = self.sq[self.rot("sq", 2)]
                P.A("activation", out=sq[:], in_=self.x[:, k, _tcs(tc)], func=AF.Square)
                P.mm(ps[:], self.ones[:], sq[:], start=(k == 0), stop=(k == 7))
            r = self.rst[self.rot("rst", 2)]
            self.rstd_from(ps, r)
            for k in range(8):
                t = self.tmp[self.rot("tmp", 2)]
                P.V("scalar_tensor_tensor", out=t[:], in0=self.x[:, k, _tcs(tc)], scalar=self.mA[:, s, k, c:c + 1],
                    in1=r[:], op0=ALU.mult, op1=ALU.mult)
                P.A("activation", out=h[:, k, _tcs(tc)], in_=t[:], func=AF.Identity, bias=self.mB(s, k, c), scale=1.0)
        return h

    def proj_fm(self, kind, nslab, ncol_chunks, nk, rhs_fn, ybuf, s):
        P = self.P
        ssb = [self.ps[4], self.ps[5], self.ps[6]]
        pend_ss = None
        for sl in range(nslab):
            off = self.ws_get(kind)
            slab = self.view(off, BF16, nk, 128 * ncol_chunks)
            for tc in range(3):
                for dd in range(ncol_chunks):
                    d = ncol_chunks * sl + dd
                    ps = self.ps[self.rot("py", 4)]
                    for k in range(nk):
                        P.mm(ps[:], slab[:, k, 128 * dd:128 * dd + 128], rhs_fn(k, tc), start=(k == 0), stop=(k == nk - 1))
                    if pend_ss is not None:
                        P.mm(*pend_ss[0], **pend_ss[1])
                    if sl == nslab - 1:
                        P.A("activation", out=ybuf[:, d, _tcs(tc)], in_=ps[:], func=AF.Copy)
                    else:
                        P.V("tensor_copy", out=ybuf[:, d, _tcs(tc)], in_=ps[:])
                    sq = self.sq[self.rot("sq", 2)]
                    P.A("activation", out=sq[:], in_=ps[:], func=AF.Square)
                    pend_ss = ((ssb[tc][:], self.ones[:], sq[:]), dict(start=(d == 0), stop=(d == 7), skip_group_check=True))
                if sl == nslab - 1:
                    P.mm(*pend_ss[0], **pend_ss[1])
                    pend_ss = None
                    if tc < 2:
                        self.post_tc((ybuf, ssb, s, self.mG), tc)
        self.pending_post = (ybuf, ssb, s, self.mG, (2,))

    def ffn(self, i, a, s):
        import os
        P = self.P
        h = self.prenorm(s)
        if os.environ.get("KSUB", "") == "pre":
            self.dbg_dump(h.rearrange("p a b -> p (a b)"), 8 * NT, BF16)
            raise StopBuild()
        gT = self.view(BIG_OFF, BF16, 22, NT)
        for sl in range(11):
            off = self.ws_get("w1")
            slab = self.view(off, BF16, 8, 512)
            for tc in range(3):
                for pp in range(2):
                    f = 2 * sl + pp
                    r = self.rot("pg", 3)
                    pg, pu = self.ps[2 * r], self.ps[2 * r + 1]
                    for k in range(8):
                        P.mm(pg[:], slab[:, k, 128 * pp:128 * pp + 128], h[:, k, _tcs(tc)], start=(k == 0), stop=(k == 7))
                    for k in range(8):
                        P.mm(pu[:], slab[:, k, 256 + 128 * pp:256 + 128 * pp + 128], h[:, k, _tcs(tc)],
                             start=(k == 0), stop=(k == 7))
                    t = self.tmp[self.rot("tmp", 2)]
                    P.A("activation", out=t[:], in_=pg[:], func=AF.Silu)
                    P.V("tensor_tensor", out=gT[:, f, _tcs(tc)], in0=t[:], in1=pu[:], op=ALU.mult)
            if i + 1 < self.n_layers and sl < 9 and not (a == 1 and i % 2 == 0):
                self.mod_step(i + 1, 9 * a + sl)
            if i == 0 and a == 0:
                self.mod_step(0, 6 + sl)
                if sl == 10:
                    self.mod_step(0, 17)
                    self.mod_finish(0, ss=(1, 2))
        if a == 1 and i + 1 < self.n_layers:
            self.mod_finish(i + 1)
        if os.environ.get("KSUB", "") == "w1":
            self.dbg_dump(gT.rearrange("p a b -> p (a b)"), 22 * NT, BF16)
            raise StopBuild()
        ybuf = h
        self.proj_fm("w2", 4, 2, 22, lambda k, tc: gT[:, k, _tcs(tc)], ybuf, s)

    def fourier(self, i):
        P = self.P
        j = i // 2
        h = self.prenorm(1)
        Vb = self.view(BIG_OFF, BF16, 12, 2048)
        UO = BIG_OFF + 49152
        pend_u = []

        def chan_dft(u, tc, g):
            for hh in range(2):
                pv = self.ps[4 + self.rot("pv", 2)]
                for tl in range(2):
                    t4 = 2 * hh + tl
                    P.mm(pv[:, 256 * tl:256 * tl + 256], u[:, 128 * t4:128 * t4 + 128], self.ccsc[:],
                         start=True, stop=True, skip_group_check=True)
                tt0 = 4 * tc + 2 * hh
                P.V("tensor_copy", out=Vb[:, tt0:tt0 + 2, 256 * g:256 * g + 256],
                    in_=pv[:].rearrange("p (a b) -> p a b", a=2))

        for sl in range(2):
            off = self.ws_get("win")
            slab = self.view(off, BF16, 8, 512)
            for gg in range(4):
                g = 4 * sl + gg
                for tc in range(3):
                    ps = self.ps[self.rot("py", 4)]
                    for k in range(8):
                        P.mm(ps[:], slab[:, k, 128 * gg:128 * gg + 128], h[:, k, _tcs(tc)], start=(k == 0), stop=(k == 7))
                    if pend_u:
                        chan_dft(*pend_u.pop(0))
                    u = self.view(UO + 1024 * self.rot("u", 4), BF16, 512)
                    P.A("activation", out=u, in_=ps[:], func=AF.Copy)
                    pend_u.append((u, tc, g))
                if g % 2 == 1:
                    while pend_u:
                        chan_dft(*pend_u.pop(0))
                    r = g // 2
                    P.dma("sync", self.ag_in[0].ap().rearrange("(a p) n -> p a n", p=128),
                          Vb[:, 4:12, 512 * r:512 * r + 512])
                    self.allgather(r, q="sync")
                    if r % 2 == 1:
                        self.gates.add(("ag", i, r // 2))
        fT = h
        if i + 1 < self.n_layers:
            for ms in range(9, 18):
                self.mod_step(i + 1, ms)
        for sq_ in range(2):
            for g in range(8):
                ps = self.ps[self.rot("py", 4)]
                n = 0
                for a in range(2):
                    for cs in range(2):
                        P.mm(ps[:, 0:256], Vb[:, 2 * sq_ + a, 256 * g + 128 * cs:256 * g + 128 * cs + 128],
                             self.d256[:, a, cs, :], start=(n == 0), stop=(n == 3))
                        n += 1
                P.V("tensor_copy", out=fT[:, g, 256 * sq_:256 * sq_ + 256], in_=ps[:, 0:256])
        for ph in range(2):
            for tt in range(32):
                off = self.ws_get("dft")
                vt = self.view(off, BF16, 1024)
                cst = self.view(off + 2048, BF16, 2, 1024)
                for gg in range(4):
                    for tq in range(2):
                        ps = self.ps[2 * gg + tq]
                        P.mm(ps[:], vt[:, 256 * gg:256 * gg + 128], cst[:, 0, 512 * tq:512 * tq + 512],
                             start=(tt == 0), stop=False)
                        P.mm(ps[:], vt[:, 256 * gg + 128:256 * gg + 256], cst[:, 1, 512 * tq:512 * tq + 512],
                             start=False, stop=(tt == 31))
            for gg in range(4):
                for tq in range(2):
                    ps = self.ps[2 * gg + tq]
                    o = fT[:, 4 * ph + gg, 512 + 512 * tq:512 + 512 * tq + 512]
                    if tq == 0:
                        P.V("tensor_copy", out=o, in_=ps[:])
                    else:
                        P.A("activation", out=o, in_=ps[:], func=AF.Copy)
        ybuf = self.view(BIG_OFF, BF16, 8, NT)
        self.proj_fm("wout", 2, 4, 8, lambda k, tc: fT[:, k, _tcs(tc)], ybuf, 1)

    def allgather(self, r, q="sync"):
        P = self.P
        ai, ao = self.ag_in[r % 2].ap(), self.ag_out[r % 2].ap()

        def cc(e):
            return e.collective_compute("AllGather", ALU.bypass, replica_groups=[[0, 1, 2, 3], [4, 5, 6, 7]],
                                        ins=[ai], outs=[ao])
        P.add("gpsimd", "cc", reads=[ai], writes=[ao], custom=cc)
        P.dma(q, self.vall.ap()[:, 512 * r:512 * r + 512], ao)

    def attention(self, i):
        P = self.P
        j = i // 2
        h = self.prenorm(1)
        B = BIG_OFF
        qT = self.view(B, BF16, 8, NT)
        Vext = self.view(B + 24576, BF16, 12, 1024)
        kTp = self.view(B + 49152, BF16, 8, 512)
        Vp = self.view(B + 57344, BF16, 4, 1024)
        kTc = self.view(B + 49152, BF16, 8, 256)
        Vc = self.view(B + 53248, BF16, 2, 1024)
        kText = self.view(HB_OFF, BF16, 8, NT)
        colmask = self.colmask_s[:].unsqueeze(1).broadcast_to([128, 22, 64])
        Ebuf = [self.sq[0][:], self.sq[1][:]]
        lg = [self.tmp[0][:], self.tmp[1][:]]
        rec = self.rst[0]
        stag32 = [self.view(B + 24576 + 2048 * t, F32, 512) for t in range(2)]
        stagb = [self.view(B + 24576 + 4096 + 1024 * t, BF16, 512) for t in range(2)]

        nk, nv = self.nk.ap(), self.nv.ap()
        for sl in (2, 3, 4, 5, 0, 1):
            off = self.ws_get("wqkv")
            slab = self.view(off, BF16, 8, 512)
            which = sl // 2
            half = sl % 2
            if which == 0 or which == 1:
                for tc in range(3 if which == 0 else 1):
                    for dd in range(4):
                        hp = 4 * half + dd
                        ps = self.ps[self.rot("py", 4)]
                        for k in range(8):
                            P.mm(ps[:], slab[:, k, 128 * dd:128 * dd + 128], h[:, k, _tcs(tc)], start=(k == 0), stop=(k == 7))
                        if which == 0:
                            P.A("activation", out=qT[:, hp, _tcs(tc)], in_=ps[:], func=AF.Copy, scale=0.125)
                        else:
                            P.V("tensor_copy", out=kTp[:, hp, 0:512], in_=ps[:])
            if which >= 1:
                dst = nk if which == 1 else nv
                for tt in range(12):
                    ps = self.ps[self.rot("py", 4)]
                    for k in range(8):
                        P.mm(ps[:], h[:, k, 128 * tt:128 * tt + 128], slab[:, k, :], start=(k == 0), stop=(k == 7))
                    if tt < 4:
                        s32 = stag32[self.rot("s32", 2)]
                        P.V("tensor_copy", out=s32, in_=ps[:])
                        sq_, t0 = tt // 2, 128 * (tt % 2)
                        d_ap = dst[sq_, j, 8 * half:8 * half + 8, t0:t0 + 128, :].rearrange("h t d -> t h d")
                        P.dma("sync", d_ap, s32.rearrange("p (h d) -> p h d", h=8))
                        if which == 2:
                            P.A("activation", out=Vp[:, tt, 512 * half:512 * half + 512], in_=ps[:], func=AF.Copy)
                    else:
                        sb_ = stagb[self.rot("sbf", 2)]
                        P.A("activation", out=sb_, in_=ps[:], func=AF.Copy)
                        P.dma("sync", self.ag_in[0].ap()[128 * (tt - 4):128 * (tt - 4) + 128, :], sb_)
                self.allgather((which - 1) * 2 + half)
        self.out_aps.append(nk)
        self.out_aps.append(nv)

        def attn_tail(pO, pD, rows, n, qslice):
            rc = self.rst[self.rot("rst", 2)][rows, 0:n]
            P.A("activation", out=rc, in_=pD[rows, 0:n], func=AF.Ln)
            P.A("activation", out=rc, in_=rc, func=AF.Exp, scale=-1.0)
            P.V("tensor_tensor", out=qslice, in0=pO[rows, 0:n], in1=rc, op=ALU.mult)

        tb0 = self.tmp[0][:].bitcast(BF16)
        Ep = [self.sq[0][:], self.sq[1][:], tb0[:, 0:512], tb0[:, 512:1024]]
        units = [(sq_, hp, e) for sq_ in range(2) for hp in range(8) for e in range(2)]

        def p_sgroup(u):
            sq_, hp, e = u
            rows = slice(64 * e, 64 * e + 64)
            qs = qT[rows, hp, 256 * sq_:256 * sq_ + 256]
            pS = self.ps[self.rot("pS", 4)]
            for kc in range(2):
                P.mm(pS[:, 256 * kc:256 * kc + 256], kTp[rows, hp, 256 * sq_ + 128 * kc:256 * sq_ + 128 * kc + 128], qs,
                     start=True, stop=True, skip_group_check=True)
            return pS

        pend = [p_sgroup(units[0]), p_sgroup(units[1])]
        for ui, u in enumerate(units):
            sq_, hp, e = u
            rows = slice(64 * e, 64 * e + 64)
            qs = qT[rows, hp, 256 * sq_:256 * sq_ + 256]
            if ui + 2 < len(units):
                pend.append(p_sgroup(units[ui + 2]))
            pS = pend[ui]
            r = self.rot("po", 2)
            pO, pD = self.ps[4 + 2 * r], self.ps[5 + 2 * r]
            E = Ep[self.rot("Ep", 4)]
            P.A("activation", out=E, in_=pS[:], func=AF.Exp)
            for kc in range(2):
                P.mm(pO[:, 0:256], Vp[:, 2 * sq_ + kc, 128 * hp:128 * hp + 128], E[:, 256 * kc:256 * kc + 256],
                     start=(kc == 0), stop=(kc == 1))
                P.mm(pD[:, 0:256], self.ones[:], E[:, 256 * kc:256 * kc + 256], start=(kc == 0), stop=(kc == 1))
            attn_tail(pO, pD, rows, 256, qs)

        ao = self.vall.ap()
        kvg0 = self.view(B + 49152, BF16, 2048)
        kvg1 = self.view(B + 53248, BF16, 2048)
        kvgs = [kvg0, kvg1]
        for et in range(12):
            kv = kvgs[et % 2]

            def gather(e, kv=kv, et=et):
                return e.indirect_dma_start(out=kv, out_offset=None, in_=ao,
                                            in_offset=bass.IndirectOffsetOnAxis(ap=self.gidx[:, et:et + 1], axis=0))
            P.add("gpsimd", "gather", reads=[ao, self.gidx[:, et:et + 1]], writes=[kv], custom=gather, is_dma=True)
            for half in range(2):
                ps = self.ps[self.rot("py", 4)]
                psb = ps[:].bitcast(BF16)
                for kk in range(4):
                    hp = 4 * half + kk
                    P.transpose(psb[:, 128 * kk:128 * kk + 128], kv[:, 128 * hp:128 * hp + 128], self.identb[:])
                P.V("tensor_copy", out=kText[:, 4 * half:4 * half + 4, 128 * et:128 * et + 128],
                    in_=psb[:, 0:512].rearrange("p (a b) -> p a b", a=4))
            P.A("activation", out=Vext[:, et, :], in_=kv[:, 1024:2048], func=AF.Copy)
        P.dma("gpsimd", kTc, self.ckT.ap()[j])
        P.dma("gpsimd", Vc, self.cv.ap()[j])

        T2rb = [self.view(B + 57344, BF16, 22, 64), self.view(B + 57344 + 2816, BF16, 22, 64)]
        Hkb = self.view(B + 57344 + 5632, BF16, 2, 16, 64)
        t0b = self.tmp[0][:].bitcast(BF16)
        t1b = self.tmp[1][:].bitcast(BF16)
        Ebuf = [self.sq[0][:], self.sq[1][:], t0b[:, 0:512], t0b[:, 512:1024], t1b[:, 0:512], t1b[:, 512:1024]]
        NE = len(Ebuf)
        P.V("memset", ap=T2rb[0], constant=0.0)
        P.V("memset", ap=T2rb[1], constant=0.0)
        cm8 = self.colmask_s[:].unsqueeze(1).broadcast_to([128, 8, 64])

        def build_steps(hd, T2r):
            for e2 in range(2):
                base = ((j * 16 + hd) * 2 + e2) * 22 * 128 + 3 * 128
                hap = bass.AP(self.rpbH, base, [[1, 64], [128, 16], [1, 64]])
                P.dma("gpsimd", Hkb[0:64, e2], hap)
            steps = []
            for e2 in range(2):
                for half in range(2):
                    def step(e2=e2, half=half, T2r=T2r):
                        ps = self.ps[self.rot("pS", 4)]
                        r2 = slice(64 * e2, 64 * e2 + 64)
                        P.mm(ps[:], self.j2b[:], Hkb[0:64, e2, 8 * half:8 * half + 8, :], start=True, stop=True)
                        P.V("tensor_tensor", out=T2r[r2, 3 + 8 * half:11 + 8 * half, :],
                            in0=ps[r2, :].rearrange("p (a b) -> p a b", a=8), in1=cm8[r2], op=ALU.add)
                    steps.append(step)
            return steps

        for st_ in build_steps(0, T2rb[0]):
            st_()
        for hd in range(16):
            hp, e = hd // 2, hd % 2
            rows = slice(64 * e, 64 * e + 64)
            T2r = T2rb[hd % 2]
            steps = build_steps(hd + 1, T2rb[(hd + 1) % 2]) if hd + 1 < 16 else []
            for b in range(2):
                QB = slice(512 + 512 * b, 1024 + 512 * b)
                qs = qT[rows, hp, QB]
                r = self.rot("po", 2)
                pO, pD = self.ps[4 + 2 * r], self.ps[5 + 2 * r]

                def sgroup(c, b=b, qs=qs, rows=rows, hp=hp, T2r=T2r):
                    pS = self.ps[self.rot("pS", 4)]
                    if c < 8:
                        ks = slice(512 * b + 128 * c, 512 * b + 128 * c + 128)
                        P.mm(pS[:], kText[rows, hp, ks], qs, start=True, stop=False)
                        P.mm(pS[:], self.identb[:], T2r[:, 14 - 2 * c:22 - 2 * c, :], start=False, stop=True)
                    else:
                        cc_ = c - 8
                        P.mm(pS[:], kTc[rows, hp, 128 * cc_:128 * cc_ + 128], qs, start=True, stop=True)
                    return pS

                pend = [sgroup(0), sgroup(1), sgroup(2)]
                for c in range(10):
                    if c + 3 < 10:
                        pend.append(sgroup(c + 3))
                    pS = pend[c]
                    E = Ebuf[self.rot("E", NE)]
                    P.A("activation", out=E, in_=pS[:], func=AF.Exp)
                    if c < 8:
                        E3 = E.rearrange("p (a b) -> p a b", a=8)
                        P.V("tensor_tensor", out=E3, in0=E3,
                            in1=self.maskt[:, b, c, :].unsqueeze(2).broadcast_to([128, 8, 64]), op=ALU.mult)
                    vsrc = Vext[:, 4 * b + c, 128 * hp:128 * hp + 128] if c < 8 else Vc[:, c - 8, 128 * hp:128 * hp + 128]
                    P.mm(pO[:], vsrc, E, start=(c == 0), stop=(c == 9))
                    P.mm(pD[:], self.ones[:], E, start=(c == 0), stop=(c == 9))
                for _ in range(2):
                    if steps:
                        steps.pop(0)()
                attn_tail(pO, pD, rows, 512, qs)
            while steps:
                steps.pop(0)()
        ybuf = self.view(B + 24576, BF16, 8, NT)
        self.proj_fm("wout", 2, 4, 8, lambda k, tc: qT[:, k, _tcs(tc)], ybuf, 1)

    def dbg_dump(self, ap, n, dt):
        d = self.nc.dram_tensor("dbg", [128, n], dt, kind="ExternalOutput")
        self.P.dma("sync", d.ap(), ap)
        self.out_aps.append(d.ap())

    def build(self):
        self.decl()
        self.alloc()
        self.out_aps = []
        self.ws_init(self.build_ws_seq())
        self.load_consts()
        self.load_x()
        import os
        stage = int(os.environ.get("KSTAGE", "9"))
        try:
            self.layers(stage)
        except StopBuild:
            pass
        self.flush_post()
        self.store_x()
        self.P.add("sync", "fence", reads=self.out_aps, custom=lambda e: None)
        self.P.emit()
        self.st.close()
        return self.nc

    def layers(self, stage):
        import os
        for i in range(self.n_layers):
            if stage < 1:
                break
            if i == 0:
                self.modulation(i)
            self.set_layer(i)
            if stage == 1 and os.environ.get("KSUB", "") == "mod":
                self.dbg_dump(self.modraw[:].rearrange("p a b -> p (a b)"), 144, F32)
                break
            self.ffn(i, 0, 0)
            if stage < 2:
                break
            if i % 2 == 0:
                self.fourier(i)
            else:
                self.attention(i)
            if stage < 3:
                break
            self.ffn(i, 1, 2)


_CONST_CACHE = {}


def _consts():
    if _CONST_CACHE:
        return _CONST_CACHE
    c = {}
    c["ident"] = np.eye(128, dtype=np.float32)
    n = np.arange(128)
    ang = 2 * np.pi * np.outer(n, n) / 128.0
    c["ccsc"] = (np.concatenate([np.cos(ang), np.sin(ang)], 1) / np.sqrt(128.0)).astype(np.float32)
    t = np.arange(256)
    ang = 2 * np.pi * np.outer(t, t) / 256.0
    d = np.stack([np.cos(ang), -np.sin(ang)], 0) / 16.0
    d = d.reshape(2, 2, 128, 256).transpose(2, 1, 0, 3)
    c["d256"] = np.ascontiguousarray(d).astype(np.float32)
    J = np.zeros((64, 64), np.float32)
    J[np.arange(64), 63 - np.arange(64)] = 1.0
    c["j2"] = np.concatenate([J, J], 1)
    qc = np.arange(64)
    cs = np.clip(qc - 8, 0, 48)
    kc = np.arange(64)
    ok = (kc[:, None] >= cs[None, :]) & (kc[:, None] < cs[None, :] + 16)
    cm = np.where(ok, 0.0, NEG).astype(np.float32)
    cm = np.concatenate([cm, cm], 0)
    c["colmask"] = np.ascontiguousarray(cm).astype(np.float32)
    ab = np.zeros((8, 8, 64), np.float32)
    for a in range(8):
        ab[a, a, :] = 1.0
    c["augB"] = ab.reshape(8, 512)
    tt = np.arange(4096)
    for q in range(4):
        tp = 1024 * q + np.arange(1024)
        ang = 2 * np.pi * ((tt[:, None] * tp[None, :]) % 4096) / 4096.0
        dd = np.stack([np.cos(ang), -np.sin(ang)], 1) / 64.0
        c["dfts%d" % q] = np.ascontiguousarray(dd.reshape(32, 128, 2, 1024)).astype(ml_dtypes.bfloat16)
        A = np.zeros((8, 2, 16, 64), np.float32)
        for b in range(2):
            for jj in range(8):
                r = 16 * q + 8 * b + jj
                rs = min(max(r - 4, 0), 56)
                for xx in range(16):
                    kr = 16 * q - 4 + 8 * b + xx
                    valid = (rs <= kr < rs + 8)
                    A[jj, b, xx, :] = 0.0 if valid else NEG
        M = np.zeros((128, 2, 8, 8), np.float32)
        for b in range(2):
            for cc in range(8):
                for e2 in range(2):
                    for jj in range(8):
                        M[64 * e2:64 * e2 + 64, b, cc, jj] = 1.0 if A[jj, b, 2 * cc + e2, 0] == 0.0 else 0.0
        c["maskt%d" % q] = M
        et = np.arange(1536)
        row = np.clip(16 * q - 4 + et // 64, 0, 63)
        idx = row * 64 + et % 64
        c["gidx%d" % q] = np.ascontiguousarray(idx.reshape(12, 128).T).astype(np.int32)
    _CONST_CACHE.update(c)
    return c


def _fm(v):
    v = np.asarray(v, np.float32)
    lead = v.shape[:-1]
    r = v.reshape(lead + (8, 128))
    r = np.moveaxis(r, -1, 0)
    return np.ascontiguousarray(r)


_NC_CACHE = {}


def kernel(x_prompt, x_sample, cache_k, cache_v, c, c_ctx, w_mod, b_mod, norm_pre, norm_post,
           ffn_w1, ffn_w2, four_w_in, four_w_out, na_w_qkv, na_w_out, na_rpb, _n_layers=DEPTH):
    f = lambda a: np.ascontiguousarray(np.asarray(a, dtype=np.float32))
    x_prompt, x_sample, cache_k, cache_v = f(x_prompt), f(x_sample), f(cache_k), f(cache_v)
    c, c_ctx, w_mod, b_mod = f(c), f(c_ctx), f(w_mod), f(b_mod)
    norm_pre, norm_post = f(norm_pre), f(norm_post)
    ffn_w1, ffn_w2, four_w_in, four_w_out = f(ffn_w1), f(ffn_w2), f(four_w_in), f(four_w_out)
    na_w_qkv, na_w_out, na_rpb = f(na_w_qkv), f(na_w_out), f(na_rpb)
    K = _consts()
    if _n_layers not in _NC_CACHE:
        _NC_CACHE[_n_layers] = Builder(_n_layers).build()
    nc = _NC_CACHE[_n_layers]

    bmT = np.ascontiguousarray(b_mod.reshape(DEPTH, 72, 128).transpose(2, 0, 1))
    gpre = np.ascontiguousarray(norm_pre.reshape(DEPTH, 3, 8, 128).transpose(3, 0, 1, 2))
    gpost = np.ascontiguousarray(norm_post.reshape(DEPTH, 3, 8, 128).transpose(3, 0, 1, 2))
    rpbH = np.zeros((2, 16, 2, 22, 128), np.float32)
    for e in range(2):
        for k in range(22):
            d = 10 - k + e
            if -7 <= d <= 7:
                rpbH[:, :, e, k, 48:79] = na_rpb[:, :, d + 7, ::-1]
    rpbH = rpbH.reshape(-1)
    in_maps = []
    for cid in range(8):
        g, q = cid // 4, cid % 4
        xin = np.concatenate([x_prompt[2 * cid], x_prompt[2 * cid + 1],
                              x_sample[g, 1024 * q:1024 * q + 1024]], 0)
        condT = np.ascontiguousarray(np.stack([c_ctx, c[g]], 0).reshape(2, 8, 128).transpose(2, 1, 0))
        ck = cache_k[g]
        ckT = np.ascontiguousarray(ck.reshape(2, 8, 2, 256, 64).transpose(0, 2, 4, 1, 3).reshape(2, 128, 8, 256))
        cvv = cache_v[g]
        cv = np.ascontiguousarray(cvv.reshape(2, 16, 2, 128, 64).transpose(0, 3, 2, 1, 4).reshape(2, 128, 2, 1024))
        in_maps.append({
            "xin": np.ascontiguousarray(xin), "condT": condT, "w_mod": w_mod, "bmT": bmT, "gpre": gpre,
            "gpost": gpost, "ffn_w1": ffn_w1, "ffn_w2": ffn_w2, "four_w_in": four_w_in,
            "four_w_out": four_w_out, "na_w_qkv": na_w_qkv, "na_w_out": na_w_out, "ckT": ckT, "cv": cv,
            "rpbH": rpbH, "ident": K["ident"], "ccsc": K["ccsc"], "d256": K["d256"],
            "dfts": K["dfts%d" % q], "j2": K["j2"], "colmask": K["colmask"], "maskt": K["maskt%d" % q], "gidx": K["gidx%d" % q],
        })
    import os
    ncores = int(os.environ.get("KCORES", "8"))
    res = run_bass_kernel_spmd(nc, in_maps[:ncores], core_ids=list(range(ncores)))
    R = list(res.results)
    if ncores < 8:
        global _DBG_R
        _DBG_R = R
        R = R + [R[0]] * (8 - ncores)
    y_prompt = np.zeros((16, 256, D), np.float32)
    y_sample = np.zeros((2, 4096, D), np.float32)
    new_k = np.zeros((16, 2, 16, 256, 64), np.float32)
    new_v = np.zeros((16, 2, 16, 256, 64), np.float32)
    for cid in range(8):
        g, q = cid // 4, cid % 4
        yo = R[cid]["yout"]
        y_prompt[2 * cid] = yo[0:256]
        y_prompt[2 * cid + 1] = yo[256:512]
        y_sample[g, 1024 * q:1024 * q + 1024] = yo[512:1536]
        new_k[2 * cid:2 * cid + 2] = R[cid]["nk"]
        new_v[2 * cid:2 * cid + 2] = R[cid]["nv"]
    return (y_prompt, y_sample, new_k, new_v)
```
